# Optimizing a Trainium2 kernel written in Bass

```python
import jax, jax.numpy as jnp
from jax import lax
import numpy as np

D_MODEL = 1024
BATCH = 2
SEQ = 8192
DEPTH = 2

GRID_W = 64
CTX_LEN = 256

ATTN_HEAD_DIM = 64
ATTN_HEADS = 8
ATTN_KV_HEADS = 2
ATTN_Q_PER_KV = ATTN_HEADS // ATTN_KV_HEADS
ATTN_WIDTH = ATTN_HEADS * ATTN_HEAD_DIM
ATTN_KV_WIDTH = ATTN_KV_HEADS * ATTN_HEAD_DIM
Q_BLOCK = 128
ROPE_THETA = 10000.0
ROPE_AXIS_DIM = ATTN_HEAD_DIM // 2

GLA_HEADS = 4
GLA_DK = 64
GLA_DV = 128
GLA_K_WIDTH = GLA_HEADS * GLA_DK
GLA_V_WIDTH = GLA_HEADS * GLA_DV
GLA_GATE_RANK = 16
GLA_GATE_TEMP = 16.0
GLA_CHUNK = 64

SGU_GROUPS = 4
SGU_CHUNK = 128
SGU_WIDTH = 512
SGU_GROUP_DIM = SGU_WIDTH // SGU_GROUPS

DEEPNORM_ALPHA = (2 * DEPTH) ** 0.25
DEEPNORM_BETA = (8 * DEPTH) ** -0.25
EPS = 1e-6

SPLIT_SIZES = (
    ATTN_WIDTH, ATTN_KV_WIDTH, ATTN_KV_WIDTH, ATTN_WIDTH,
    GLA_K_WIDTH, GLA_K_WIDTH, GLA_V_WIDTH, GLA_GATE_RANK, GLA_GATE_RANK,
    GLA_V_WIDTH,
    SGU_WIDTH, SGU_WIDTH, SGU_WIDTH,
    D_MODEL, D_MODEL, D_MODEL,
)
PROJ_WIDTH = sum(SPLIT_SIZES)

kernel_name = "hybrid_gqa_gla_sgu_prefix_trunk"


def _split_points():
    pts, acc = [], 0
    for s in SPLIT_SIZES[:-1]:
        acc += s
        pts.append(acc)
    return pts


def layer_norm(x, g, b):
    xf = x.astype(jnp.float32)
    mu = jnp.mean(xf, axis=-1, keepdims=True)
    var = jnp.mean(jnp.square(xf - mu), axis=-1, keepdims=True)
    return ((xf - mu) * lax.rsqrt(var + EPS) * g + b).astype(x.dtype)


def rms_norm(x, g):
    xf = x.astype(jnp.float32)
    y = xf * lax.rsqrt(jnp.mean(jnp.square(xf), axis=-1, keepdims=True) + EPS)
    return (y * g).astype(x.dtype)


def axial_rope_angles(rows):
    row = jnp.repeat(jnp.arange(rows, dtype=jnp.float32), GRID_W)
    col = jnp.tile(jnp.arange(GRID_W, dtype=jnp.float32), rows)
    freqs = ROPE_THETA ** (-jnp.arange(0, ROPE_AXIS_DIM, 2, dtype=jnp.float32) / ROPE_AXIS_DIM)
    return row[:, None] * freqs, col[:, None] * freqs


def rope_1d(x, ang):
    m = ang.shape[-1]
    cos = jnp.cos(ang)[None, :, None, :]
    sin = jnp.sin(ang)[None, :, None, :]
    x1, x2 = x[..., :m], x[..., m:]
    return jnp.concatenate([x1 * cos - x2 * sin, x1 * sin + x2 * cos], axis=-1)


def apply_axial_rope(x, ang_r, ang_c):
    xf = x.astype(jnp.float32)
    out = jnp.concatenate([rope_1d(xf[..., :ROPE_AXIS_DIM], ang_r),
                           rope_1d(xf[..., ROPE_AXIS_DIM:], ang_c)], axis=-1)
    return out.astype(x.dtype)


def gqa_project(pq, pk, pv, q_norm, k_norm):
    B, N, _ = pq.shape
    q = rms_norm(pq.reshape(B, N, ATTN_HEADS, ATTN_HEAD_DIM), q_norm)
    k = rms_norm(pk.reshape(B, N, ATTN_KV_HEADS, ATTN_HEAD_DIM), k_norm)
    v = pv.reshape(B, N, ATTN_KV_HEADS, ATTN_HEAD_DIM)
    return q, k, v


def attend(q, k, v):
    s = jnp.einsum('btgrd,blgd->bgrtl', q, k, preferred_element_type=jnp.float32) * (ATTN_HEAD_DIM ** -0.5)
    p = jax.nn.softmax(s, axis=-1)
    return jnp.einsum('bgrtl,blgd->btgrd', p.astype(v.dtype), v)


def blocked_attention(q, k, v):
    B, N = q.shape[:2]
    nb = N // Q_BLOCK
    qb = q.reshape(B, nb, Q_BLOCK, ATTN_KV_HEADS, ATTN_Q_PER_KV, ATTN_HEAD_DIM)
    qb = jnp.moveaxis(qb, 1, 0)
    ob = lax.map(lambda qi: attend(qi, k, v), qb)
    return jnp.moveaxis(ob, 0, 1).reshape(B, N, ATTN_WIDTH)


def gla_scan(q, k, v, g, s0):
    B, H, N, dk = q.shape
    dv = v.shape[-1]
    nc = N // GLA_CHUNK
    mask = jnp.tril(jnp.ones((GLA_CHUNK, GLA_CHUNK), dtype=bool))

    def chunks(t):
        return jnp.moveaxis(t.reshape(B, H, nc, GLA_CHUNK, t.shape[-1]), 2, 0)

    def step(s, inp):
        qc, kc, vc, gc = inp
        b = jnp.cumsum(gc, axis=2)
        b_last = b[:, :, -1:, :]
        inter = jnp.einsum('bhtd,bhde->bhte', qc * jnp.exp(b), s)
        rel = jnp.where(mask[:, :, None], b[:, :, :, None, :] - b[:, :, None, :, :], -jnp.inf)
        att = jnp.einsum('bhtd,bhsd,bhtsd->bhts', qc, kc, jnp.exp(rel))
        o = inter + jnp.einsum('bhts,bhse->bhte', att, vc)
        s_new = jnp.exp(b_last[:, :, 0, :])[..., None] * s + \
            jnp.einsum('bhsd,bhse->bhde', kc * jnp.exp(b_last - b), vc)
        return s_new, o

    s_fin, o = lax.scan(step, s0, (chunks(q), chunks(k), chunks(v), chunks(g)))
    return jnp.moveaxis(o, 0, 2).reshape(B, H, N, dv), s_fin


def gla_heads(pq, pk, pv, paf, pab, a2_f, ab_f, a2_b, ab_b):
    B, N, _ = pq.shape

    def heads(t, d):
        return t.reshape(B, N, GLA_HEADS, d).transpose(0, 2, 1, 3).astype(jnp.float32)

    def log_decay(pa, a2, ab):
        z = (pa @ a2 + ab).astype(jnp.float32)
        return heads(jax.nn.log_sigmoid(z) / GLA_GATE_TEMP, GLA_DK)

    q = heads(pq, GLA_DK) * (GLA_DK ** -0.5)
    k = heads(pk, GLA_DK)
    v = heads(pv, GLA_DV)
    return q, k, v, log_decay(paf, a2_f, ab_f), log_decay(pab, a2_b, ab_b)


def gla_finish(o, o_norm, dtype):
    B, H, N, dv = o.shape
    o = rms_norm(o.transpose(0, 2, 1, 3), o_norm)
    return o.reshape(B, N, H * dv).astype(dtype)


def sgu(pu, pv, ln_g, ln_b, w_s, b_s):
    B, N, _ = pu.shape
    nch = N // SGU_CHUNK
    vn = layer_norm(pv, ln_g, ln_b).reshape(B, nch, SGU_CHUNK, SGU_GROUPS, SGU_GROUP_DIM)
    mixed = jnp.einsum('gts,bnsgc->bntgc', w_s, vn) + b_s.T[None, None, :, :, None]
    return pu * mixed.reshape(B, N, SGU_WIDTH)


def trunk_layer(x, ctx, c, c_ctx, p, ang_r, ang_c, need_ctx):
    B = x.shape[0]
    shift, scale, gate = jnp.split(jax.nn.silu(c) @ p['w_ada'] + p['b_ada'], 3, axis=-1)
    shift_c, scale_c, gate_c = jnp.split(jax.nn.silu(c_ctx) @ p['w_ada'] + p['b_ada'], 3, axis=-1)
    h = x * (1.0 + scale[:, None, :]) + shift[:, None, :]
    hc = ctx * (1.0 + scale_c) + shift_c

    pts = _split_points()
    (a_q, a_k, a_v, a_g, g_q, g_k, g_v, g_af, g_ab, g_g,
     s_u, s_v, s_g, m_a, m_g, m_s) = jnp.split(h @ p['w_in'], pts, axis=-1)
    (ca_q, ca_k, ca_v, ca_g, cg_q, cg_k, cg_v, cg_af, cg_ab, cg_g,
     cs_u, cs_v, cs_g, cm_a, cm_g, cm_s) = jnp.split(hc @ p['w_in'], pts, axis=-1)

    ql, kl, vl = gqa_project(a_q, a_k, a_v, p['q_norm'], p['k_norm'])
    ql = apply_axial_rope(ql, ang_r, ang_c)
    kl = apply_axial_rope(kl, ang_r, ang_c)
    qc, kc, vc = gqa_project(ca_q, ca_k, ca_v, p['q_norm'], p['k_norm'])
    k_all = jnp.concatenate([kc, kl], axis=1)
    v_all = jnp.concatenate([vc, vl], axis=1)
    y_attn = blocked_attention(ql, k_all, v_all) * jax.nn.silu(a_g)

    qlg, klg, vlg, gfl, gbl = gla_heads(g_q, g_k, g_v, g_af, g_ab, p['a2_f'], p['ab_f'], p['a2_b'], p['ab_b'])
    qcg, kcg, vcg, gfc, gbc = gla_heads(cg_q, cg_k, cg_v, cg_af, cg_ab, p['a2_f'], p['ab_f'], p['a2_b'], p['ab_b'])
    zeros = jnp.zeros((B, GLA_HEADS, GLA_DK, GLA_DV), jnp.float32)
    flip = lambda t: jnp.flip(t, axis=2)
    oc_f, s_f = gla_scan(qcg, kcg, vcg, gfc, zeros)
    oc_b, s_b = gla_scan(flip(qcg), flip(kcg), flip(vcg), flip(gbc), zeros)
    ol_f, _ = gla_scan(qlg, klg, vlg, gfl, s_f)
    ol_b, _ = gla_scan(flip(qlg), flip(klg), flip(vlg), flip(gbl), s_b)
    y_gla = gla_finish(ol_f + flip(ol_b), p['gla_o_norm'], x.dtype) * jax.nn.silu(g_g)

    y_sgu = sgu(s_u, s_v, p['sgu_ln_g'], p['sgu_ln_b'], p['sgu_w'], p['sgu_b']) * jax.nn.silu(s_g)

    def merge(ya, yg, ys, ga, gg, gs):
        m = (jax.nn.sigmoid(ga) * (ya @ p['w_br_attn'])
             + jax.nn.sigmoid(gg) * (yg @ p['w_br_gla'])
             + jax.nn.sigmoid(gs) * (ys @ p['w_br_sgu']))
        return m @ p['w_out']

    out = merge(y_attn, y_gla, y_sgu, m_a, m_g, m_s)
    x_new = layer_norm(DEEPNORM_ALPHA * x + gate[:, None, :] * out, p['post_g'], p['post_b'])

    if not need_ctx:
        return x_new, ctx

    L = ctx.shape[1]
    yc_attn = attend(qc.reshape(B, L, ATTN_KV_HEADS, ATTN_Q_PER_KV, ATTN_HEAD_DIM), kc, vc)
    yc_attn = yc_attn.reshape(B, L, ATTN_WIDTH) * jax.nn.silu(ca_g)
    yc_gla = gla_finish(oc_f + flip(oc_b), p['gla_o_norm'], ctx.dtype) * jax.nn.silu(cg_g)
    yc_sgu = sgu(cs_u, cs_v, p['sgu_ln_g'], p['sgu_ln_b'], p['sgu_w'], p['sgu_b']) * jax.nn.silu(cs_g)
    out_c = merge(yc_attn, yc_gla, yc_sgu, cm_a, cm_g, cm_s)
    ctx_new = layer_norm(DEEPNORM_ALPHA * ctx + gate_c * out_c, p['post_g'], p['post_b'])
    return x_new, ctx_new


def setup_inputs(seed: int = 0) -> dict:
    key = jax.random.key(seed)
    ks = iter(jax.random.split(key, 32))

    def nrm(shape, s):
        return jax.random.normal(next(ks), shape, jnp.float32) * s

    L, D = DEPTH, D_MODEL
    return {
        "x": nrm((BATCH, SEQ, D), 1.0),
        "c": nrm((BATCH, D), 1.0),
        "ctx": nrm((BATCH, CTX_LEN, D), 1.0),
        "c_ctx": nrm((D,), 1.0),
        "w_ada": nrm((L, D, 3 * D), 0.5 * D ** -0.5),
        "b_ada": nrm((L, 3 * D), 0.02),
        "w_in": nrm((L, D, PROJ_WIDTH), D ** -0.5),
        "attn_q_norm": 1.0 + nrm((L, ATTN_HEAD_DIM), 0.02),
        "attn_k_norm": 1.0 + nrm((L, ATTN_HEAD_DIM), 0.02),
        "gla_a2_f": nrm((L, GLA_GATE_RANK, GLA_K_WIDTH), GLA_GATE_RANK ** -0.5),
        "gla_ab_f": nrm((L, GLA_K_WIDTH), 0.1),
        "gla_a2_b": nrm((L, GLA_GATE_RANK, GLA_K_WIDTH), GLA_GATE_RANK ** -0.5),
        "gla_ab_b": nrm((L, GLA_K_WIDTH), 0.1),
        "gla_o_norm": 1.0 + nrm((L, GLA_DV), 0.02),
        "sgu_ln_g": 1.0 + nrm((L, SGU_WIDTH), 0.02),
        "sgu_ln_b": nrm((L, SGU_WIDTH), 0.02),
        "sgu_w": nrm((L, SGU_GROUPS, SGU_CHUNK, SGU_CHUNK), SGU_CHUNK ** -0.5),
        "sgu_b": 1.0 + nrm((L, SGU_GROUPS, SGU_CHUNK), 0.02),
        "w_br_attn": nrm((L, ATTN_WIDTH, D), ATTN_WIDTH ** -0.5 * DEEPNORM_BETA),
        "w_br_gla": nrm((L, GLA_V_WIDTH, D), GLA_V_WIDTH ** -0.5 * DEEPNORM_BETA),
        "w_br_sgu": nrm((L, SGU_WIDTH, D), SGU_WIDTH ** -0.5 * DEEPNORM_BETA),
        "w_out": nrm((L, D, D), D ** -0.5 * DEEPNORM_BETA),
        "post_ln_g": 1.0 + nrm((L, D), 0.02),
        "post_ln_b": nrm((L, D), 0.02),
    }


def reference(x, c, ctx, c_ctx, w_ada, b_ada, w_in, attn_q_norm, attn_k_norm,
              gla_a2_f, gla_ab_f, gla_a2_b, gla_ab_b, gla_o_norm,
              sgu_ln_g, sgu_ln_b, sgu_w, sgu_b,
              w_br_attn, w_br_gla, w_br_sgu, w_out, post_ln_g, post_ln_b):
    n_lat = x.shape[1]
    rows = n_lat // GRID_W
    ang_r, ang_c = axial_rope_angles(rows)
    for i in range(DEPTH):
        p = {
            'w_ada': w_ada[i], 'b_ada': b_ada[i], 'w_in': w_in[i],
            'q_norm': attn_q_norm[i], 'k_norm': attn_k_norm[i],
            'a2_f': gla_a2_f[i], 'ab_f': gla_ab_f[i], 'a2_b': gla_a2_b[i], 'ab_b': gla_ab_b[i],
            'gla_o_norm': gla_o_norm[i],
            'sgu_ln_g': sgu_ln_g[i], 'sgu_ln_b': sgu_ln_b[i], 'sgu_w': sgu_w[i], 'sgu_b': sgu_b[i],
            'w_br_attn': w_br_attn[i], 'w_br_gla': w_br_gla[i], 'w_br_sgu': w_br_sgu[i],
            'w_out': w_out[i], 'post_g': post_ln_g[i], 'post_b': post_ln_b[i],
        }
        x, ctx = trunk_layer(x, ctx, c, c_ctx, p, ang_r, ang_c, need_ctx=(i < DEPTH - 1))
    return x
```

```python
import contextlib
import numpy as np
import ml_dtypes
import concourse.bass as bass
import concourse.mybir as mybir
from concourse.bass_utils import run_bass_kernel_spmd

F32 = mybir.dt.float32
BF16 = mybir.dt.bfloat16
AF = mybir.ActivationFunctionType
ALU = mybir.AluOpType
AX = mybir.AxisListType

D = 1024
KC = 8
NT = 16
NCT = 2
NS = NT + NCT
TOK = NS * 128
NKT = 2 + 64
AQ, AK, AV, AG, GQ, GK, GV, GA, GG, SU, SV, SG, MA, MG, MS, WEND = (
    0, 512, 640, 768, 1280, 1536, 1792, 2304, 2368, 2880, 3392, 3904, 4416, 5440, 6464, 7488)
ALPHA = 4 ** 0.25
EPS = 1e-6


class _Op:
    __slots__ = ("eng", "fn", "deps", "is_dma", "marked", "tok", "slot_wait", "raw")


class Prog:
    ENGS = ("pe", "act", "dve", "pool", "sp")
    NDMA = 8

    def __init__(self, nc, st):
        self.nc = nc
        self.sems = {}
        for e in self.ENGS:
            self.sems[("c", e, 0)] = st.enter_context(nc.semaphore("c_" + e))
            if e in ("sp", "pool", "act"):
                for s in range(self.NDMA):
                    self.sems[("d", e, s)] = st.enter_context(nc.semaphore("d_%s%d" % (e, s)))
        self.ccount = {e: 0 for e in self.ENGS}
        self.ndma = {e: 0 for e in self.ENGS}


class Sched:
    def __init__(self, prog):
        self.p = prog
        self.ops = {e: [] for e in Prog.ENGS}
        self.last_w = {}
        self.readers = {}

    def add(self, eng, fn, reads=(), writes=(), dma=False, raw=False):
        op = _Op()
        op.raw = raw
        op.eng, op.fn, op.is_dma, op.marked = eng, fn, dma, False
        op.tok = None
        op.slot_wait = None
        deps = []
        for r in reads:
            lw = self.last_w.get(r)
            if lw is not None:
                deps.append(lw)
        for w in writes:
            lw = self.last_w.get(w)
            if lw is not None:
                deps.append(lw)
            deps.extend(self.readers.get(w, ()))
            self.readers[w] = []
            self.last_w[w] = op
        for r in reads:
            self.readers.setdefault(r, []).append(op)
        dd = []
        seen = set()
        for d in deps:
            if d is op or id(d) in seen:
                continue
            seen.add(id(d))
            if (not d.is_dma) and d.eng == "pe" and eng == "pe" and not dma:
                continue
            d.marked = True
            dd.append(d)
        op.deps = dd
        if dma:
            n = self.p.ndma[eng]
            self.p.ndma[eng] = n + 1
            K = Prog.NDMA
            op.tok = ("d", eng, n % K, 16 * (n // K + 1))
            op.slot_wait = 16 * (n // K)
        self.ops[eng].append(op)
        return op

    def emit(self):
        p = self.p
        nc = p.nc
        for e in Prog.ENGS:
            c = p.ccount[e]
            for op in self.ops[e]:
                if op.is_dma:
                    continue
                if op.marked:
                    c += 1
                    op.tok = ("c", e, 0, c)
            p.ccount[e] = c
        sems = p.sems
        K = Prog.NDMA
        with nc.Block() as block:
            dec = {"pe": block.tensor, "act": block.scalar, "dve": block.vector,
                   "pool": block.gpsimd, "sp": block.sync}
            for e in Prog.ENGS:
                ops = self.ops[e]
                if not ops:
                    continue

                def body(engine, ops=ops, e=e):
                    waited = {}
                    for op in ops:
                        need = {}
                        for d in op.deps:
                            k = d.tok[:3]
                            v = d.tok[3]
                            if need.get(k, 0) < v:
                                need[k] = v
                        if op.is_dma and op.slot_wait:
                            k = op.tok[:3]
                            if need.get(k, 0) < op.slot_wait:
                                need[k] = op.slot_wait
                        for k, v in need.items():
                            if waited.get(k, 0) >= v:
                                continue
                            waited[k] = v
                            engine.wait_ge(sems[k], v)
                        ins = op.fn(engine)
                        if op.raw:
                            continue
                        if op.is_dma:
                            ins.then_inc(sems[op.tok[:3]], 16)
                        elif op.marked:
                            ins.then_inc(sems[op.tok[:3]], 1)
                    n = p.ndma[e]
                    if n:
                        for s in range(K):
                            cnt = (n - s + K - 1) // K if n > s else 0
                            if cnt:
                                engine.wait_ge(sems[("d", e, s)], 16 * cnt)

                dec[e](body)


class Builder:
    def __init__(self):
        self.phase = "B"
        self.layer1 = True
        self.dyn = False
        self.jv = None
        self.nc = bass.Bass("TRN2", target_bir_lowering=False)
        self.gst = contextlib.ExitStack()
        self.prog = Prog(self.nc, self.gst)
        self.banks = [self.gst.enter_context(self.nc.psum_tensor("bank%d" % i, [128, 512], F32))
                      for i in range(8)]
        self.bi = 0
        self.S = None
        self.sst = None
        self.uid = 0

    def din(self, name, shape, dt=F32):
        return self.nc.dram_tensor(name, list(shape), dt, kind="ExternalInput").ap()

    def dout(self, name, shape, dt=F32):
        return self.nc.dram_tensor(name, list(shape), dt, kind="ExternalOutput").ap()

    def begin(self):
        self.S = Sched(self.prog)
        self.sst = contextlib.ExitStack()
        self.set_banks(range(8))
        if self.dyn:
            self.S.add("sp", self._load_jv, raw=True)

    def end(self):
        self.S.emit()
        self.sst.close()
        self.S = None

    def sb(self, name, shape, dt=F32, persistent=False):
        st = self.gst if persistent else self.sst
        self.uid += 1
        return st.enter_context(self.nc.sbuf_tensor("sb%d_%s" % (self.uid, name), list(shape), dt))

    def set_banks(self, lst):
        self.bank_list = list(lst)
        self.bi = 0

    def bank(self):
        i = self.bank_list[self.bi % len(self.bank_list)]
        self.bi += 1
        return self.banks[i], "bank%d" % i

    def rbank(self, i):
        return self.banks[i], "bank%d" % i

    def mm(self, out, lhsT, rhs, start, stop, r, w):
        self.S.add("pe", lambda e: e.matmul(out, lhsT=lhsT, rhs=rhs, start=start, stop=stop), reads=r, writes=w)

    def tr(self, out, in_, ident, r, w):
        self.S.add("pe", lambda e: e.transpose(out, in_=in_, identity=ident), reads=r, writes=w)

    def act(self, out, in_, func, r, w, bias=None, scale=None):
        kw = {}
        if bias is not None:
            kw["bias"] = bias
        if scale is not None:
            kw["scale"] = scale
        self.S.add("act", lambda e: e.activation(out=out, in_=in_, func=func, **kw), reads=r, writes=w)

    def tt(self, out, in0, in1, op, r, w, eng="dve"):
        self.S.add(eng, lambda e: e.tensor_tensor(out=out, in0=in0, in1=in1, op=op), reads=r, writes=w)

    def ts(self, out, in0, s1, s2, op0, op1, r, w, eng="dve"):
        if op1 is None:
            self.S.add(eng, lambda e: e.tensor_scalar(out=out, in0=in0, scalar1=s1, scalar2=None, op0=op0), reads=r, writes=w)
        else:
            self.S.add(eng, lambda e: e.tensor_scalar(out=out, in0=in0, scalar1=s1, scalar2=s2, op0=op0, op1=op1), reads=r, writes=w)

    def stt(self, out, in0, scalar, in1, op0, op1, r, w):
        self.S.add("dve", lambda e: e.scalar_tensor_tensor(out=out, in0=in0, scalar=scalar, in1=in1, op0=op0, op1=op1), reads=r, writes=w)

    def cp(self, out, in_, r, w, eng="dve"):
        if eng == "act":
            self.S.add("act", lambda e: e.copy(out=out, in_=in_), reads=r, writes=w)
        else:
            self.S.add(eng, lambda e: e.tensor_copy(out=out, in_=in_), reads=r, writes=w)

    def memset(self, ap, val, w, eng="dve"):
        self.S.add(eng, lambda e: e.memset(ap, val), writes=w)

    def dma(self, out, in_, r, w, eng="sp"):
        def fn(e):
            o = out() if callable(out) else out
            i = in_() if callable(in_) else in_
            return e.dma_start(out=o, in_=i)
        self.S.add(eng, fn, reads=r, writes=w, dma=True)

    def _load_jv(self, e):
        if self.jv is None:
            self.jv = e.partition_id() % 4
        return None

    def chunk_rows(self, base, c, r0, r1):
        if c == "dyn":
            return lambda: base[bass.ds(self.jv, 1), r0:r1, :].rearrange("o p c -> p (o c)")
        return base[c, r0:r1, :]

    def declare(self):
        I32 = mybir.dt.int32
        dc = {}
        dc["x"] = self.din("x", [4, NT * 128, D])
        dc["ctx"] = self.din("ctx", [NCT * 128, D])
        dc["cvec"] = self.din("cvec", [128, KC, 2])
        dc["consts"] = self.din("consts", [128, 3, 128])
        dc["ropec_all"] = self.din("ropec_all", [128, 4 * NT, 64])
        dc["ropes_all"] = self.din("ropes_all", [128, 4 * NT, 64])
        dc["ropec_own"] = self.din("ropec_own", [128, NT, 64])
        dc["ropes_own"] = self.din("ropes_own", [128, NT, 64])
        dc["x_out"] = self.dout("x_out", [NT * 128, D])
        nc = self.nc
        dc["kvk"] = nc.dram_tensor("kvk", [128, 64 * 128], BF16, kind="Internal").ap()
        dc["kvv"] = nc.dram_tensor("kvv", [128, 64, 192], BF16, kind="Internal").ap()
        dc["gstate"] = nc.dram_tensor("gstate", [4, 128, 2, 2, 256], F32, kind="Internal").ap()
        dc["x1s"] = nc.dram_tensor("x1s", [4, NT * 128, D], F32, kind="Internal").ap()
        dc["ctx1s"] = nc.dram_tensor("ctx1s", [NCT * 128, D], F32, kind="Internal").ap()
        dc["x1own"] = nc.dram_tensor("x1own", [1, NT * 128, D], F32, kind="Internal").ap()
        dc["gown"] = nc.dram_tensor("gown", [1, 128, 2, 2, 256], F32, kind="Internal").ap()
        dc["sfall"] = nc.dram_tensor("sfall", [4, NT, 128, 2, 256], F32, kind="Internal").ap()
        dc["kvb"] = nc.dram_tensor("kvb", [NCT + 4 * NT, 128, 2, 257], F32, kind="Internal").ap()
        dc["sfown"] = nc.dram_tensor("sfown", [1, NT, 128, 2, 256], F32, kind="Internal").ap()
        self.dcom = dc
        self.dl = []
        for l in range(2):
            sfx = "_%d" % l
            d = {}
            for name, shape in (("w_ada", [D, 3 * D]), ("b_ada_l", [128, 16]), ("b_gate", [1, D]), ("w_in", [D, WEND]),
                                ("k_norm", [1, 64]), ("q_norm", [1, 64]), ("a2b", [64, 256]), ("o_norm", [1, 128]),
                                ("sgu_g", [1, 512]), ("sgu_b", [1, 512]), ("sgu_wT", [4, 128, 128]), ("sgu_bT", [128, 4]),
                                ("w_br", [3, 512, D]), ("w_out", [D, D]), ("post_g", [1, D]), ("post_b", [1, D])):
                d[name] = self.din(name + sfx, shape)
            d["cvec"] = dc["cvec"]
            d["consts"] = dc["consts"]
            d["kt_all"] = dc["kvk"]
            d["v_all"] = dc["kvv"]
            self.dl.append(d)

    def build(self):
        self.declare()
        B = self
        self.cst = B.sb("cst", [128, 3, 128], F32, True)
        self.identb = B.sb("identb", [128, 128], BF16, True)
        self.ones1 = B.sb("ones1", [128, 1], F32, True)
        self.mod = B.sb("mod", [128, 16, 2], F32, True)
        self.gateb = B.sb("gateb", [128, 2, D], F32, True)
        self.a2b = B.sb("a2bt", [64, 256], F32, True)
        dc = self.dcom
        for l in range(2):
            self.L = l
            self.d = self.dl[l]
            self.layer1 = (l == 0)
            self.xsrc = dc["x"] if l == 0 else dc["x1s"]
            self.csrc = dc["ctx"] if l == 0 else dc["ctx1s"]
            self.dyn = False
            self.gsrc = dc["gstate"]
            self.sfsrc = dc["sfall"]
            self.stage_ada()
            self.phaseA()
            if l == 0:
                for c in range(4):
                    self.phaseB(c, do_ctx=(c == 0))
            else:
                self.stage_dynsel()
                self.xsrc = dc["x1own"]
                self.gsrc = dc["gown"]
                self.sfsrc = dc["sfown"]
                self.phaseB(0, do_ctx=False, rope_c="dyn")
        self.gst.close()
        return self.nc

    def phaseA(self):
        B = self
        scope = contextlib.ExitStack()
        self.hT = scope.enter_context(self.nc.sbuf_tensor("sb_hTall_%d" % self.L, [128, KC, (NCT + 4 * NT) * 128], BF16))
        slots = list(range(NCT + 4 * NT))

        def src(slot):
            if slot < NCT:
                return self.csrc[slot * 128:(slot + 1) * 128, :]
            t = slot - NCT
            return self.xsrc[t // NT, (t % NT) * 128:(t % NT + 1) * 128, :]
        self.stage_hT(slots, src)
        for c in range(4):
            self.stage_attn_kv(c)
        self.stage_gla("A", None, False)
        scope.close()

    def stage_dynsel(self):
        B, dc = self, self.dcom
        self.dyn = True
        B.begin()
        for q in range(4):
            r0, r1 = q * 512, (q + 1) * 512
            B.dma(dc["x1own"][0, r0:r1, :], self.chunk_rows(dc["x1s"], "dyn", r0, r1), [], [])
        B.dma(dc["gown"][0], lambda: dc["gstate"][bass.ds(self.jv, 1), :, :, :, :].rearrange("o p a b c -> p (o a) b c"), [], [])
        for q in range(4):
            B.dma(dc["sfown"][0, q * 4:(q + 1) * 4].rearrange("t p a b -> p t a b"),
                  lambda q=q: dc["sfall"][bass.ds(self.jv, 1), q * 4:(q + 1) * 4, :, :, :].rearrange("o t p a b -> p (o t) a b"), [], [])
        B.end()
        self.dyn = False

    def phaseB(self, c, do_ctx, rope_c=None):
        B = self
        if rope_c is None:
            rope_c = c
        tag = "%d_%s" % (self.L, c)
        self.main = contextlib.ExitStack()
        self.hT = self.main.enter_context(self.nc.sbuf_tensor("sb_hT_" + tag, [128, KC, TOK], BF16))
        self.yTa = self.main.enter_context(self.nc.sbuf_tensor("sb_yTa_" + tag, [128, 4, TOK], BF16))

        def src(slot):
            if slot < NCT:
                return self.csrc[slot * 128:(slot + 1) * 128, :]
            t = slot - NCT
            return self.chunk_rows(self.xsrc, c, t * 128, (t + 1) * 128)
        self.src_rows = src
        self.stage_attn(rope_c, do_ctx, hT_src=src)
        self.yTg = self.main.enter_context(self.nc.sbuf_tensor("sb_yTg_" + tag, [128, 4, TOK], BF16))
        self.stage_gla("B", c, do_ctx)
        self.yTs = self.main.enter_context(self.nc.sbuf_tensor("sb_yTs_" + tag, [128, 4, TOK], BF16))
        self.stage_sgu(do_ctx)
        self.mT = self.main.enter_context(self.nc.sbuf_tensor("sb_mT_" + tag, [128, KC, TOK], BF16))
        self.stage_merge1(do_ctx)
        self.stage_merge2(c, do_ctx)
        self.main.close()

    def stage_ada(self):
        B, d = self, self.d
        need_gate = True
        B.begin()
        cv = B.sb("cv", [128, KC, 2])
        sc = B.sb("sc", [128, KC, 2])
        wk = [B.sb("wk%d" % i, [128, 3 * D]) for i in range(2)]
        bada = B.sb("bada", [128, 16])
        B.dma(self.cst[:], d["consts"], [], ["cst"])
        B.dma(self.a2b[:], d["a2b"], [], ["a2b"])
        B.cp(self.identb[:], self.cst[:, 0, :], ["cst"], ["identb"])
        B.memset(self.ones1[:], 1.0, ["ones1"])
        B.dma(cv[:], d["cvec"], [], ["cv"])
        B.dma(bada[:], d["b_ada_l"], [], ["bada"])
        B.act(sc[:], cv[:], AF.Silu, ["cv"], ["sc"])
        psA, rA = B.bank()
        psAv = psA[:, 0:32].rearrange("p (o t) -> p o t", t=2)
        if need_gate:
            scb = B.sb("scb", [128, KC, 2, 128])
            bg = B.sb("bg", [128, D])
            B.dma(bg[:], d["b_gate"].broadcast_to([128, D]), [], ["bg"])
            B.cp(scb[:], sc[:].unsqueeze(3).broadcast_to([128, KC, 2, 128]), ["sc"], ["scb"])
            psG = [B.bank() for _ in range(4)]
        for k in range(KC):
            w = wk[k % 2]
            wr = "wk%d" % (k % 2)
            B.dma(w[:], d["w_ada"][k * 128:(k + 1) * 128, :], [], [wr])
            for oc in range(16):
                B.mm(psAv[:, oc, :], w[:, oc * 128:(oc + 1) * 128], sc[:, k, :], k == 0 and oc == 0, k == KC - 1 and oc == 15,
                     [wr, "sc"], [rA])
            if need_gate:
                for which in range(2):
                    for half in range(2):
                        pg, rg = psG[which * 2 + half]
                        B.mm(pg[:, :], scb[:, k, which, :], w[:, 2048 + half * 512:2048 + (half + 1) * 512],
                             k == 0, k == KC - 1, [wr, "scb"], [rg])
        B.tt(self.mod[:], psAv, bada[:].unsqueeze(2).broadcast_to([128, 16, 2]), ALU.add, [rA, "bada"], ["mod"])
        B.ts(self.mod[:, 8:16, :], self.mod[:, 8:16, :], 1.0, None, ALU.add, None, ["mod"], ["mod"])
        if need_gate:
            for which in range(2):
                for half in range(2):
                    pg, rg = psG[which * 2 + half]
                    B.tt(self.gateb[:, which, half * 512:(half + 1) * 512], pg[:, :], bg[:, half * 512:(half + 1) * 512],
                         ALU.add, [rg, "bg"], ["gateb"])
        B.end()

    def stage_hT(self, slots, src, embedded=False):
        B = self
        if not embedded:
            B.begin()
        xt = [B.sb("xt%d" % i, [128, D]) for i in range(3)]
        for n, slot in enumerate(slots):
            x = xt[n % 3]
            xr = "xt%d" % (n % 3)
            which = 1 if slot < NCT else 0
            B.dma(x[:], src(slot), [], [xr])
            for half in range(2):
                ps, rp = B.bank()
                for j in range(4):
                    k = half * 4 + j
                    B.tr(ps[:, j * 128:(j + 1) * 128], x[:, k * 128:(k + 1) * 128], self.cst[:, 0, :], [xr, "cst"], [rp])
                for j in range(4):
                    k = half * 4 + j
                    if half == 0:
                        B.ts(self.hT[:, k, slot * 128:(slot + 1) * 128], ps[:, j * 128:(j + 1) * 128],
                             self.mod[:, 8 + k, which:which + 1], self.mod[:, k, which:which + 1], ALU.mult, ALU.add,
                             [rp, "mod"], ["hT%d" % slot])
                    else:
                        B.act(self.hT[:, k, slot * 128:(slot + 1) * 128], ps[:, j * 128:(j + 1) * 128], AF.Identity,
                              [rp, "mod"], ["hT%d" % slot], bias=self.mod[:, k, which:which + 1], scale=self.mod[:, 8 + k, which:which + 1])
        if not embedded:
            B.end()

    def loadw(self, dst, c0, n, res):
        src = self.d["w_in"].rearrange("(k p) c -> p k c", p=128)[:, :, c0:c0 + n]
        self.dma(dst, src, [], [res], eng="pool")

    def proj_tm(self, slot, wt, wres, c0, n, ps, rp):
        for k in range(KC):
            self.mm(ps[:, 0:n], self.hT[:, k, slot * 128:(slot + 1) * 128], wt[:, k, c0:c0 + n], k == 0, k == KC - 1,
                    ["hT%d" % slot, wres], [rp])

    def proj_fm(self, slot0, nslots, wt, wres, c0, m, ps, rp):
        ntok = nslots * 128
        for k in range(KC):
            self.mm(ps[0:m, 0:ntok], wt[:, k, c0:c0 + m], self.hT[:, k, slot0 * 128:slot0 * 128 + ntok], k == 0, k == KC - 1,
                    ["hT%d" % s for s in range(slot0, slot0 + nslots)] + [wres], [rp])

    def qk_prep(self, ps, rp, nh, gain_b, rope_idx, out, rout, scr):
        pset = dict(s1=self.s1, s2=self.s2, s3=self.s3, sm=self.sm, sfx="")
        for _ in self.qk_prep_g(ps, rp, nh, gain_b, rope_idx, out, rout, pset):
            pass

    def qk_prep_g(self, ps, rp, nh, gain_b, rope_idx, out, rout, pset):
        B = self
        W = nh * 64
        s1, s2, s3, sm, x = pset["s1"], pset["s2"], pset["s3"], pset["sm"], pset["sfx"]
        R = lambda n: n + x
        B.tt(s1[:, 0:W], ps[:, 0:W], ps[:, 0:W], ALU.mult, [rp], [R("qs1")])
        yield
        B.S.add("dve", lambda e: e.tensor_reduce(out=sm[:, 0:nh], in_=s1[:, 0:W].rearrange("p (h d) -> p h d", h=nh),
                                                axis=AX.X, op=ALU.add), reads=[R("qs1")], writes=[R("qsm")])
        B.ts(sm[:, 8:8 + nh], sm[:, 0:nh], 1.0 / 64.0, EPS, ALU.mult, ALU.add, [R("qsm")], [R("qsm2")])
        B.act(sm[:, 16:16 + nh], sm[:, 8:8 + nh], AF.Ln, [R("qsm2")], [R("qsm3")])
        B.act(sm[:, 24:24 + nh], sm[:, 16:16 + nh], AF.Exp, [R("qsm3")], [R("qsm4")], scale=-0.5)
        yield
        B.tt(s2[:, 0:W].rearrange("p (h d) -> p h d", h=nh), ps[:, 0:W].rearrange("p (h d) -> p h d", h=nh),
             sm[:, 24:24 + nh].unsqueeze(2).broadcast_to([128, nh, 64]), ALU.mult, [rp, R("qsm4")], [R("qs2")])
        if rope_idx is None:
            B.tt(out.rearrange("p (h d) -> p h d", h=nh), s2[:, 0:W].rearrange("p (h d) -> p h d", h=nh),
                 gain_b[:].unsqueeze(1).broadcast_to([128, nh, 64]), ALU.mult, [R("qs2"), "gain"], [rout])
            return
        B.tt(s1[:, 0:W].rearrange("p (h d) -> p h d", h=nh), s2[:, 0:W].rearrange("p (h d) -> p h d", h=nh),
             gain_b[:].unsqueeze(1).broadcast_to([128, nh, 64]), ALU.mult, [R("qs2"), "gain", R("qs1")], [R("qs1")])
        cosF = self.ropec[:, rope_idx, :]
        sinF = self.ropes[:, rope_idx, :]
        B.tt(s2[:, 0:W].rearrange("p (h d) -> p h d", h=nh), s1[:, 0:W].rearrange("p (h d) -> p h d", h=nh),
             cosF.unsqueeze(1).broadcast_to([128, nh, 64]), ALU.mult, [R("qs1"), "rope"], [R("qs2")])
        q5 = s1[:, 0:W].rearrange("p (h a f i) -> p h a f i", h=nh, a=2, f=2)
        t5 = s3[:, 0:W].rearrange("p (h a f i) -> p h a f i", h=nh, a=2, f=2)
        sn = sinF.rearrange("p (a f i) -> p a f i", a=2, f=2)
        for f in range(2):
            B.tt(t5[:, :, :, f, :], q5[:, :, :, 1 - f, :],
                 sn[:, :, f, :].unsqueeze(1).broadcast_to([128, nh, 2, 16]), ALU.mult, [R("qs1"), "rope"], [R("qs3_%d" % f)],
                 eng="pool")
        B.tt(out, s2[:, 0:W], s3[:, 0:W], ALU.add, [R("qs2"), R("qs3_0"), R("qs3_1")], [rout])

    def prep_common(self, rope_c):
        B, d = self, self.d
        dc = self.dcom
        if rope_c == "dyn":
            rc_src, rs_src = dc["ropec_own"], dc["ropes_own"]
        else:
            rc_src = dc["ropec_all"][:, rope_c * NT:(rope_c + 1) * NT, :]
            rs_src = dc["ropes_all"][:, rope_c * NT:(rope_c + 1) * NT, :]
        self.s1 = B.sb("s1", [128, 512])
        self.s2 = B.sb("s2", [128, 512])
        self.s3 = B.sb("s3", [128, 512])
        self.sm = B.sb("sm", [128, 32])
        self.stg = B.sb("stg", [128, 512])
        self.ropec = B.sb("ropec", [128, NT, 64])
        self.ropes = B.sb("ropes", [128, NT, 64])
        self.gk = B.sb("gk", [128, 64])
        B.dma(self.ropec[:], rc_src, [], ["rope"])
        B.dma(self.ropes[:], rs_src, [], ["rope"])
        B.dma(self.gk[:], d["k_norm"].broadcast_to([128, 64]), [], ["gain"])

    def kv_slot(self, slot, wt, wres, KT, ktres, col0, V, vres, vt, ridx):
        B = self
        ps, rp = B.bank()
        B.proj_tm(slot, wt, wres, AK - self.wbase, 256, ps, rp)
        B.cp(self.stg[:, 0:256], ps[:, 0:256], [rp], ["stg"])
        ps, rp = self.stg, "stg"
        kb = self.kbf
        B.qk_prep(ps, rp, 2, self.gk, ridx, kb[:], "kbf", (self.s1, self.s2, self.sm))
        B.cp(V[:, vt, 0:64], ps[:, 128:192], [rp], [vres], eng="act")
        B.cp(V[:, vt, 128:192], ps[:, 192:256], [rp], [vres], eng="act")
        pt, rt = B.bank()
        ptb = pt[:].bitcast(BF16)
        B.tr(ptb[:, 0:128], kb[:], self.identb[:], ["kbf", "identb"], [rt])
        B.cp(KT[:, col0:col0 + 128], ptb[:, 0:128], [rt], [ktres])

    def stage_attn_kv(self, c):
        B, d = self, self.d
        B.begin()
        self.prep_common(c)
        wt = B.sb("wkv", [128, KC, 256], BF16)
        self.wbase = AK
        B.loadw(wt[:], AK, 256, "wkv")
        KT = B.sb("KTo", [128, NT * 128], BF16)
        V = B.sb("Vo", [128, NT, 192], BF16)
        B.memset(V[:], 1.0, ["Vo"])
        NSET = 4
        psets = [dict(s1=B.sb("ks1", [128, 128]), s2=B.sb("ks2", [128, 128]), s3=B.sb("ks3", [128, 128]), sm=B.sb("ksm", [128, 32]),
                      stg=B.sb("kstg", [128, 256]), kb=B.sb("kkb", [128, 128], BF16), sfx="_%d" % i) for i in range(NSET)]

        def slot_gen(i):
            slot = NCT + c * NT + i
            pset = psets[i % NSET]
            x = pset["sfx"]
            stg, kb = pset["stg"], pset["kb"]
            ps, rp = B.bank()
            B.proj_tm(slot, wt, "wkv", 0, 256, ps, rp)
            B.cp(stg[:], ps[:, 0:256], [rp], ["kstg" + x])
            yield
            B.cp(V[:, i, 0:64], stg[:, 128:192], ["kstg" + x, "Vo"], ["Vo%d" % i], eng="act")
            B.cp(V[:, i, 128:192], stg[:, 192:256], ["kstg" + x, "Vo"], ["Vo%d" % i], eng="act")
            for _ in B.qk_prep_g(stg, "kstg" + x, 2, self.gk, i, kb[:], "kkb" + x, pset):
                yield
            yield
            pt, rt = B.bank()
            ptb = pt[:].bitcast(BF16)
            B.tr(ptb[:, 0:128], kb[:], self.identb[:], ["kkb" + x, "identb"], [rt])
            B.cp(KT[:, i * 128:(i + 1) * 128], ptb[:, 0:128], [rt], ["KTo%d" % i])

        B.pipeline([slot_gen(i) for i in range(NT)], NSET)
        B.dma(self.dcom["kvk"][:, c * NT * 128:(c + 1) * NT * 128], KT[:], ["KTo%d" % i for i in range(NT)], [])
        B.dma(self.dcom["kvv"][:, c * NT:(c + 1) * NT, :], V[:], ["Vo"] + ["Vo%d" % i for i in range(NT)], [])
        B.end()

    def stage_attn(self, c, do_ctx, hT_src=None):
        B, d, l1 = self, self.d, do_ctx
        B.begin()
        self.prep_common(c)
        self.kbf = B.sb("kbf", [128, 128], BF16)
        gq = B.sb("gq", [128, 64])
        B.dma(gq[:], d["q_norm"].broadcast_to([128, 64]), [], ["gain"])
        wt = B.sb("watt", [128, KC, 1280], BF16)
        self.wbase = 0
        B.loadw(wt[:], 0, 1280, "watt")
        KT = B.sb("KT", [128, NKT * 128], BF16)
        V = B.sb("V", [128, NKT, 192], BF16)
        B.memset(V[:, 0:2, :], 1.0, ["Vc"])
        B.dma(KT[:, 256:], d["kt_all"], [], ["KTl"], eng="act")
        B.dma(V[:, 2:, :], d["v_all"], [], ["Vl"], eng="act")
        if hT_src is not None:
            self.stage_hT(list(range(NS)), hT_src, embedded=True)
            B.set_banks(range(8))
        for slot in range(NCT):
            B.kv_slot(slot, wt, "watt", KT, "KTc", slot * 128, V, "Vc", slot, None)
        qbf = B.sb("qbf", [128, 512], BF16)
        QT = [B.sb("QT%d" % i, [128, 2, 4, 512], BF16) for i in range(2)]
        for i in range(2):
            B.memset(QT[i][:], 0.0, ["QT%d" % i])
        pT = [B.sb("pT%d" % i, [128, 512], BF16) for i in range(4)]
        tmp = B.sb("atmp", [128, 512])
        rden = B.sb("rden", [128, 512])
        yTa = self.yTa
        groups = []
        if l1:
            groups.append((0, 2, 0, 2))
        for g in range(4):
            groups.append((NCT + 4 * g, 4, 0, NKT))
        pcount = 0
        pocount = 0

        def make_pieces(gi):
            slot0, nsl, k0, k1 = groups[gi]
            ntok = nsl * 128
            tok0 = slot0 * 128
            qt = QT[gi % 2]
            qres = "QT%d" % (gi % 2)
            pcs = []
            for s in range(slot0, slot0 + nsl):
                pcs.append(lambda s=s: q_piece(s, slot0, qt, qres))
            return pcs

        def g_piece(gi, p4, slot0, nsl):
            ntok = nsl * 128
            tok0 = slot0 * 128
            ps, rp = B.bank()
            B.proj_fm(slot0, nsl, wt, "watt", AG + p4 * 128, 128, ps, rp)
            B.act(yTa[:, p4, tok0:tok0 + ntok], ps[:, 0:ntok], AF.Silu, [rp], ["yTa_%d_%d" % (gi, p4)])

        def q_piece(s, slot0, qt, qres):
            if True:
                ps, rp = B.bank()
                B.proj_tm(s, wt, "watt", AQ, 512, ps, rp)
                B.cp(self.stg[:, :], ps[:, :], [rp], ["stg"])
                ps, rp = self.stg, "stg"
                B.qk_prep(ps, rp, 8, gq, (s - NCT) if s >= NCT else None, qbf[:], "qbf", (self.s1, self.s2, self.sm))
                pt, rt = B.bank()
                ptb = pt[:].bitcast(BF16)
                for p4 in range(4):
                    B.tr(ptb[:, p4 * 128:(p4 + 1) * 128], qbf[:, p4 * 128:(p4 + 1) * 128], self.identb[:], ["qbf", "identb"], [rt])
                for hh_ in range(2):
                    B.cp(qt[64 * hh_:64 * hh_ + 64, hh_, :, (s - slot0) * 128:(s - slot0 + 1) * 128],
                         ptb[64 * hh_:64 * hh_ + 64, 0:512].rearrange("p (a t) -> p a t", a=4), [rt], [qres])

        B.set_banks(range(8))
        for gi_, (slot0_, nsl_, _k0, _k1) in enumerate(groups):
            for p4 in range(4):
                g_piece(gi_, p4, slot0_, nsl_)
        for pc in make_pieces(0):
            pc()
        for gi, (slot0, nsl, k0, k1) in enumerate(groups):
            ntok = nsl * 128
            tok0 = slot0 * 128
            qt = QT[gi % 2]
            qres = "QT%d" % (gi % 2)
            nxt = make_pieces(gi + 1) if gi + 1 < len(groups) else []
            for p4 in range(4):
                for hh in range(2):
                    r0 = 64 * hh
                    B.set_banks(range(2, 8))
                    po, ro = B.rbank(pocount % 2)
                    pocount += 1
                    kts = list(range(k0, k1))
                    nk = len(kts)
                    sbanks = {}

                    def qk(i):
                        kt = kts[i]
                        psb, rs = B.bank()
                        sbanks[i] = (psb, rs)
                        B.mm(psb[:, 0:ntok], KT[:, kt * 128:(kt + 1) * 128], qt[:, hh, p4, 0:ntok], True, True,
                             [qres, "KTc" if kt < 2 else "KTl"], [rs])

                    LA = 3
                    for i in range(min(LA, nk)):
                        qk(i)
                    for i in range(nk):
                        kt = kts[i]
                        psb, rs = sbanks.pop(i)
                        pt_ = pT[pcount % 4]
                        pr = "pT%d" % (pcount % 4)
                        pcount += 1
                        B.act(pt_[:, 0:ntok], psb[:, 0:ntok], AF.Exp, [rs], [pr], scale=0.125)
                        if i + LA < nk:
                            qk(i + LA)
                        B.mm(po[:, 0:ntok], V[:, kt, r0:r0 + 128], pt_[:, 0:ntok], i == 0, i == nk - 1,
                             [pr, "Vc" if kt < 2 else "Vl"], [ro])
                    n0, d0 = (0, 64) if hh == 0 else (64, 0)
                    yres = "yTa_%d_%d" % (gi, p4)
                    B.tt(tmp[n0:n0 + 64, 0:ntok], po[n0:n0 + 64, 0:ntok], yTa[n0:n0 + 64, p4, tok0:tok0 + ntok], ALU.mult,
                         [ro, yres], ["atmp"])
                    B.S.add("dve", lambda e, n0=n0, d0=d0, po=po, ntok=ntok: e.reciprocal(out=rden[n0:n0 + 64, 0:ntok], in_=po[d0:d0 + 64, 0:ntok]),
                            reads=[ro], writes=["rden"])
                    B.tt(yTa[n0:n0 + 64, p4, tok0:tok0 + ntok], tmp[n0:n0 + 64, 0:ntok], rden[n0:n0 + 64, 0:ntok], ALU.mult,
                         ["rden", "atmp"], [yres], eng="pool")
                    if nxt:
                        nxt.pop(0)()
            while nxt:
                nxt.pop(0)()
        B.end()

    def pipeline(self, gens, depth):
        active = []
        it = iter(gens)
        pending = next(it, None)
        while pending is not None or active:
            if pending is not None and len(active) < depth:
                active.append(pending)
                pending = next(it, None)
            for g in list(active):
                try:
                    next(g)
                except StopIteration:
                    active.remove(g)

    def stage_gla(self, ph, c, do_ctx):
        B, d, l1 = self, self.d, do_ctx
        B.begin()
        if ph == "A":
            wt = B.sb("wgla", [128, KC, 832], BF16)
            B.loadw(wt[:], GK, 832, "wgla")
            wb = GK
        else:
            wt = B.sb("wgla", [128, KC, 1600], BF16)
            B.loadw(wt[:], GQ, 1600, "wgla")
            wb = GQ
        Lmat = [self.cst[:, 1, :], self.cst[:, 2, :]]
        NBUF = 3
        B.set_banks(range(6))
        outs_needed = (ph == "B")
        sets = []
        for i in range(NBUF):
            bs = {"i": i}
            bs["paT"] = B.sb("paT", [64, 128])
            B.memset(bs["paT"][:], 1.0, ["paT%d" % i])
            bs["kT"] = B.sb("gkT", [128, 2, 128])
            bs["vt"] = B.sb("gv", [128, 512], BF16)
            bs["lz"] = B.sb("lz", [128, 2, 256])
            bs["ez"] = B.sb("ez", [128, 2, 256])
            bs["exh"] = bs["ez"][:].rearrange("p d (a t) -> p d a t", a=2)
            bs["ntot"] = B.sb("ntot", [128, 2, 2, 1])
            bs["ac"] = B.sb("ac", [128, 2, 2, 1])
            bs["KhT"] = B.sb("KhT", [128, 2, 2, 128], BF16)
            bs["Kh"] = B.sb("Kh", [128, 2, 2, 128], BF16)
            if outs_needed:
                bs["qT"] = B.sb("gqT", [128, 2, 128])
                bs["exq"] = B.sb("exq", [128, 2, 128])
                bs["exk"] = B.sb("exk", [128, 2, 128])
                bs["Qt"] = B.sb("Qt", [128, 2, 2, 2, 128], BF16)
                B.memset(bs["Qt"][:], 0.0, ["Qt%d" % i])
                bs["Kt"] = B.sb("Kt", [128, 2, 2, 128], BF16)
                bs["attm"] = B.sb("attm", [128, 8, 128], BF16)
                bs["gsl"] = B.sb("gsl", [128, 512])
            sets.append(bs)
        Sst = B.sb("Sst", [128, 2, 2, 256])
        Sbf = B.sb("Sbf", [128, 2, 2, 256], BF16)
        Atot = B.sb("Atot", [128, 2, 2, 1])
        if outs_needed:
            Sf_all = B.sb("Sf_all", [128, NS, 2, 256], BF16)
            onb = B.sb("onb", [128, 128])
            B.dma(onb[:], d["o_norm"].broadcast_to([128, 128]), [], ["onb"])
            osb = B.sb("osb", [128, 512])
            osq = B.sb("osq", [128, 512])
            osm = B.sb("osm", [128, 16])
            ybf = B.sb("ybf", [128, 512], BF16)
        counter = [0]

        def tile(slot, dirs, out_dirs, state_dirs, pre_state=None, save_sf=False, dump_dirs=()):
            tno = counter[0]
            counter[0] += 1
            bs = sets[tno % NBUF]
            i = bs["i"]
            R = lambda n: "%s%d" % (n, i)
            kT, vt, paT, lz, ez = bs["kT"], bs["vt"], bs["paT"], bs["lz"], bs["ez"]
            psk, rk = B.bank()
            for pr in range(2):
                B.proj_fm(slot, 1, wt, "wgla", GK - wb + pr * 128, 128, psk[:, pr * 128:(pr + 1) * 128], rk)
            B.cp(kT[:], psk[:, 0:256].rearrange("p (a t) -> p a t", a=2), [rk], [R("kT")], eng="act")
            if out_dirs:
                qT = bs["qT"]
                psq, rq = B.bank()
                for pr in range(2):
                    B.proj_fm(slot, 1, wt, "wgla", GQ - wb + pr * 128, 128, psq[:, pr * 128:(pr + 1) * 128], rq)
                B.ts(qT[:], psq[:, 0:256].rearrange("p (a t) -> p a t", a=2), 0.125, None, ALU.mult, None, [rq], [R("qT")])
            psv, rv = B.bank()
            B.proj_tm(slot, wt, "wgla", GV - wb, 512, psv, rv)
            B.cp(vt[:], psv[:, :], [rv], [R("vt")], eng="act")
            psa, ra = B.bank()
            B.proj_fm(slot, 1, wt, "wgla", GA - wb, 64, psa, ra)
            B.cp(paT[0:16, :], psa[0:16, 0:128], [ra], [R("paT")])
            B.cp(paT[32:48, :], psa[32:48, 0:128], [ra], [R("paT")])
            if out_dirs:
                psg, rg = B.bank()
                B.proj_tm(slot, wt, "wgla", GG - wb, 512, psg, rg)
                B.act(bs["gsl"][:], psg[:, :], AF.Silu, [rg], [R("gsl")])
            yield
            for dr in dirs:
                psz, rz = B.bank()
                B.mm(psz[:, 0:256], paT[32 * dr:32 * dr + 17, :], self.a2b[32 * dr:32 * dr + 17, :], True, True,
                     [R("paT"), "a2b"], [rz])
                B.act(ez[:, dr, :], psz[:, 0:256], AF.Exp, [rz], [R("ez%d" % dr)], scale=-1.0)
                B.act(lz[:, dr, :], ez[:, dr, :], AF.Ln, [R("ez%d" % dr)], [R("lz%d" % dr)], bias=1.0)
            yield
            for dr in dirs:
                lr = R("lz%d" % dr)
                psc, rc = B.bank()
                for pr in range(2):
                    B.mm(psc[:, pr * 128:(pr + 1) * 128], lz[:, dr, pr * 128:(pr + 1) * 128], Lmat[dr], True, True, [lr, "cst"], [rc])
                    B.mm(psc[:, 256 + pr:257 + pr], lz[:, dr, pr * 128:(pr + 1) * 128], self.ones1[:], True, True, [lr, "ones1"], [rc])
                if dr in state_dirs:
                    B.act(bs["ntot"][:, dr, :, 0], psc[:, 256:258], AF.Identity, [rc], [R("ntot")], scale=-1.0 / 16.0)
                    B.act(bs["ac"][:, dr, :, 0], psc[:, 256:258], AF.Exp, [rc], [R("ac")], scale=-1.0 / 16.0)
                cum = psc[:, 0:256].rearrange("p (a t) -> p a t", a=2)
                if dr in out_dirs:
                    exq, exk, Qt, Kt, qT = bs["exq"], bs["exk"], bs["Qt"], bs["Kt"], bs["qT"]
                    B.act(exq[:], cum, AF.Exp, [rc], [R("exq")], scale=-1.0 / 16.0)
                    B.act(exk[:], cum, AF.Exp, [rc], [R("exk")], scale=1.0 / 16.0)
                    for hh_ in range(2):
                        ps_ = slice(64 * hh_, 64 * hh_ + 64)
                        B.tt(Qt[ps_, dr, :, hh_, :], qT[ps_, :, :], exq[ps_, :, :], ALU.mult, [R("qT"), R("exq")], [R("Qt")])
                    B.tt(Kt[:, dr, :, :], kT[:], exk[:], ALU.mult, [R("kT"), R("exk")], [R("Kt")])
                if dr in state_dirs:
                    exh = bs["exh"]
                    for pr in range(2):
                        B.act(exh[:, dr, pr, :], psc[:, pr * 128:(pr + 1) * 128], AF.Exp, [rc, R("ntot")], [R("exh%d" % dr), R("ez%d" % dr)], scale=1.0 / 16.0,
                              bias=bs["ntot"][:, dr, pr, :])
                    B.tt(bs["KhT"][:, dr], kT[:], exh[:, dr], ALU.mult, [R("kT"), R("exh%d" % dr), R("ez%d" % dr)], [R("KhT%d" % dr)])
            yield
            for dr in out_dirs:
                pat, rat = B.bank()
                for pr in range(2):
                    for hh in range(2):
                        h = pr * 2 + hh
                        B.mm(pat[:, h * 128:(h + 1) * 128], bs["Kt"][:, dr, pr, :], bs["Qt"][:, dr, pr, hh, :], True, True,
                             [R("Kt"), R("Qt")], [rat])
                B.tt(bs["attm"][:, dr * 4:(dr + 1) * 4, :], pat[:, :].rearrange("p (h t) -> p h t", h=4),
                     Lmat[dr].unsqueeze(1).broadcast_to([128, 4, 128]), ALU.mult, [rat, "cst"], [R("attm%d" % dr)])
            for dr in state_dirs:
                pst, rt = B.bank()
                ptb = pst[:].bitcast(BF16)
                for pr in range(2):
                    B.tr(ptb[:, pr * 128:(pr + 1) * 128], bs["KhT"][:, dr, pr, :], self.identb[:], [R("KhT%d" % dr), "identb"], [rt])
                B.cp(bs["Kh"][:, dr], ptb[:, 0:256].rearrange("p (a t) -> p a t", a=2), [rt], [R("Kh%d" % dr)])
            yield
            if pre_state is not None:
                pre_state()
            if out_dirs:
                pso, rso = B.rbank(6 + tno % 2)
                n_mm = 0
                tot_mm = len(out_dirs) * 8
                for dr in out_dirs:
                    for h in range(4):
                        pr, hh = h // 2, h % 2
                        if dr in state_dirs:
                            srhs = Sbf[:, dr, pr, hh * 128:(hh + 1) * 128]
                            sres = "Sbf%d" % dr
                        else:
                            srhs = Sf_all[:, slot, pr, hh * 128:(hh + 1) * 128]
                            sres = "Sf%d" % slot
                        B.mm(pso[:, h * 128:(h + 1) * 128], bs["attm"][:, dr * 4 + h, :], vt[:, h * 128:(h + 1) * 128],
                             n_mm == 0, False, [R("attm%d" % dr), R("vt")], [rso])
                        n_mm += 1
                        B.mm(pso[:, h * 128:(h + 1) * 128], bs["Qt"][:, dr, pr, hh, :], srhs,
                             False, n_mm == tot_mm - 1, [R("Qt"), sres], [rso])
                        n_mm += 1
            for dr in state_dirs:
                if save_sf:
                    B.cp(Sf_all[:, slot], Sst[:, dr], ["Sst%d" % dr], ["Sf%d" % slot])
                if dr in dump_dirs:
                    k_ = tno % 2
                    kres = "kvst%d" % k_
                    for pr in range(2):
                        pkv, rkv = B.bank()
                        B.mm(pkv[:, 0:256], bs["Kh"][:, dr, pr, :], vt[:, pr * 256:(pr + 1) * 256], True, True, [R("Kh%d" % dr), R("vt")], [rkv])
                        B.cp(kvst[k_][:, pr, 0:256], pkv[:, 0:256], [rkv], [kres])
                    B.cp(kvst[k_][:, :, 256], bs["ac"][:, dr, :, 0], [R("ac")], [kres])
                    B.dma(self.dcom["kvb"][slot], kvst[k_][:], [kres], ["kvb%d" % slot])
                    continue
                for pr in range(2):
                    pkv, rkv = B.bank()
                    B.mm(pkv[:, 0:256], bs["Kh"][:, dr, pr, :], vt[:, pr * 256:(pr + 1) * 256], True, True, [R("Kh%d" % dr), R("vt")], [rkv])
                    B.stt(Sst[:, dr, pr, :], Sst[:, dr, pr, :], bs["ac"][:, dr, pr, :], pkv[:, 0:256], ALU.mult, ALU.add,
                          [rkv, R("ac"), "Sst%d" % dr], ["Sst%d" % dr])
                B.cp(Sbf[:, dr], Sst[:, dr], ["Sst%d" % dr], ["Sbf%d" % dr], eng="pool")
            yield
            if out_dirs:
                B.cp(osb[:], pso[:, :], [rso], ["osb"])
                B.tt(osq[:], osb[:], osb[:], ALU.mult, ["osb"], ["osq"])
                B.S.add("dve", lambda e: e.tensor_reduce(out=osm[:, 0:4], in_=osq[:].rearrange("p (h d) -> p h d", h=4),
                                                        axis=AX.X, op=ALU.add), reads=["osq"], writes=["osm"])
                B.ts(osm[:, 4:8], osm[:, 0:4], 1.0 / 128.0, EPS, ALU.mult, ALU.add, ["osm"], ["osm2"])
                B.act(osm[:, 8:12], osm[:, 4:8], AF.Ln, ["osm2"], ["osm3"])
                B.act(osm[:, 12:16], osm[:, 8:12], AF.Exp, ["osm3"], ["osm4"], scale=-0.5)
                B.tt(osq[:].rearrange("p (h d) -> p h d", h=4), osb[:].rearrange("p (h d) -> p h d", h=4),
                     osm[:, 12:16].unsqueeze(2).broadcast_to([128, 4, 128]), ALU.mult, ["osb", "osm4", "osq"], ["osq"])
                B.tt(osb[:].rearrange("p (h d) -> p h d", h=4), osq[:].rearrange("p (h d) -> p h d", h=4),
                     onb[:].unsqueeze(1).broadcast_to([128, 4, 128]), ALU.mult, ["osq", "onb", "osb"], ["osb"])
                B.tt(ybf[:], osb[:], bs["gsl"][:], ALU.mult, ["osb", R("gsl")], ["ybf"])
                pt, rt2 = B.bank()
                ptb2 = pt[:].bitcast(BF16)
                for c4 in range(4):
                    B.tr(ptb2[:, c4 * 128:(c4 + 1) * 128], ybf[:, c4 * 128:(c4 + 1) * 128], self.identb[:], ["ybf", "identb"], [rt2])
                B.cp(self.yTg[:, :, slot * 128:(slot + 1) * 128], ptb2[:, 0:512].rearrange("p (a t) -> p a t", a=4), [rt2],
                     ["yTg%d" % slot])

        def zero_state(dr):
            B.memset(Sst[:, dr], 0.0, ["Sst%d" % dr])
            B.memset(Sbf[:, dr], 0.0, ["Sbf%d" % dr], eng="pool")

        B.memset(Atot[:], 1.0, ["Atot"])
        gstate = self.gsrc
        ctxs = list(range(NCT))
        if ph == "A":
            zero_state(0)
            zero_state(1)
            sfst = [B.sb("sfst", [128, 2, 256]) for _ in range(2)]
            kvst = [B.sb("kvst", [128, 2, 257]) for _ in range(2)]
            ring = [B.sb("kvring", [128, 2, 257]) for _ in range(3)]
            fwd = [(s_, None) for s_ in ctxs]
            for cc in range(4):
                for i in range(NT):
                    def hook(cc=cc, i=i):
                        k_ = (cc * NT + i) % 2
                        B.cp(sfst[k_][:], Sst[:, 0], ["Sst0"], ["sfst%d" % k_])
                        if i == 0:
                            B.dma(gstate[cc, :, 0, :, :], sfst[k_][:], ["sfst%d" % k_], [])
                        B.dma(self.dcom["sfall"][cc, i], sfst[k_][:], ["sfst%d" % k_], [])
                    fwd.append((NCT + cc * NT + i, hook))
            B.pipeline([tile(sf, [0, 1], [], [0, 1], pre_state=hf, dump_dirs=(1,)) for (sf, hf) in fwd], NBUF)
            order = list(reversed(ctxs)) + [NCT + cc * NT + i for cc in range(3, -1, -1) for i in range(NT - 1, -1, -1)]
            for idx, slot in enumerate(order):
                rb = ring[idx % 3]
                rres = "kvring%d" % (idx % 3)
                B.dma(rb[:], self.dcom["kvb"][slot], ["kvb%d" % slot], [rres])
                if slot >= NCT and (slot - NCT) % NT == NT - 1:
                    cc = (slot - NCT) // NT
                    k_ = cc % 2
                    B.cp(sfst[k_][:], Sst[:, 1], ["Sst1"], ["sfst%d" % k_])
                    B.dma(gstate[cc, :, 1, :, :], sfst[k_][:], ["sfst%d" % k_], [])
                for pr in range(2):
                    B.stt(Sst[:, 1, pr, :], Sst[:, 1, pr, :], rb[:, pr, 256:257], rb[:, pr, 0:256], ALU.mult, ALU.add,
                          [rres, "Sst1"], ["Sst1"])
        else:
            own = list(range(NCT, NS))
            if l1:
                zero_state(0)
                zero_state(1)
                B.pipeline([tile(s_, [0], [], [0], save_sf=True) for s_ in ctxs], NBUF)
                B.pipeline([tile(s_, [0, 1], [0, 1], [1]) for s_ in reversed(ctxs)], NBUF)
            B.dma(Sst[:, 1], gstate[c, :, 1, :, :], [], ["Sst1"])
            B.cp(Sbf[:, 1], Sst[:, 1], ["Sst1"], ["Sbf1"], eng="pool")
            B.dma(Sf_all[:, NCT:NS], self.sfsrc[c].rearrange("t p a b -> p t a b"), [], ["Sf%d" % s_ for s_ in own], eng="pool")
            B.pipeline([tile(s_, [0, 1], [0, 1], [1]) for s_ in reversed(own)], NBUF)
        B.end()

    def stage_sgu(self, do_ctx):
        B, d, l1 = self, self.d, do_ctx
        B.begin()
        wt = B.sb("wsgu", [128, KC, 1536], BF16)
        B.loadw(wt[:], SU, 1536, "wsgu")
        wsT32 = B.sb("wsT32", [128, 4, 128])
        wsT = B.sb("wsT", [128, 4, 128], BF16)
        B.dma(wsT32[:], d["sgu_wT"].rearrange("g s t -> s g t"), [], ["wsT32"])
        B.cp(wsT[:], wsT32[:], ["wsT32"], ["wsT"])
        bT = B.sb("bT", [128, 4])
        B.dma(bT[:], d["sgu_bT"], [], ["bT"])
        lg = B.sb("lg", [128, 512])
        lb = B.sb("lb", [128, 512])
        B.dma(lg[:], d["sgu_g"].broadcast_to([128, 512]), [], ["lg"])
        B.dma(lb[:], d["sgu_b"].broadcast_to([128, 512]), [], ["lb"])
        NBUF = 4
        sets = []
        for i in range(NBUF):
            sets.append(dict(i=i, st6=B.sb("st6", [128, 6]), mv=B.sb("mv", [128, 4]), vn=B.sb("vn", [128, 512]),
                             vnb=B.sb("vnb", [128, 512], BF16), u=B.sb("u", [128, 512]), gs=B.sb("gs", [128, 512]),
                             ybf=B.sb("sybf", [128, 512], BF16)))
        slots = list(range(NS)) if l1 else list(range(NCT, NS))

        def tile(n, slot):
            bs = sets[n % NBUF]
            R = lambda nm: "%s%d" % (nm, bs["i"])
            st6, mv, vn, vnb, u, gs, ybf = bs["st6"], bs["mv"], bs["vn"], bs["vnb"], bs["u"], bs["gs"], bs["ybf"]
            psu, ru = B.bank()
            B.proj_tm(slot, wt, "wsgu", 0, 512, psu, ru)
            psv, rv = B.bank()
            B.proj_tm(slot, wt, "wsgu", 512, 512, psv, rv)
            psg, rg = B.bank()
            B.proj_tm(slot, wt, "wsgu", 1024, 512, psg, rg)
            B.act(gs[:], psg[:, :], AF.Silu, [rg], [R("gs")])
            B.cp(vn[:], psv[:, :], [rv], [R("vn")], eng="act")
            B.tt(u[:], psu[:, :], gs[:], ALU.mult, [ru, R("gs")], [R("u")])
            yield
            B.S.add("dve", lambda e: e.bn_stats(out=st6[:], in_=vn[:]), reads=[R("vn")], writes=[R("st6")])
            B.S.add("dve", lambda e: e.bn_aggr(out=mv[:, 0:2], in_=st6[:]), reads=[R("st6")], writes=[R("mv")])
            B.ts(mv[:, 2:3], mv[:, 1:2], EPS, None, ALU.add, None, [R("mv")], [R("mv2")])
            B.act(mv[:, 3:4], mv[:, 2:3], AF.Ln, [R("mv2")], [R("mv3")])
            B.act(mv[:, 2:3], mv[:, 3:4], AF.Exp, [R("mv3"), R("mv2")], [R("mv4")], scale=-0.5)
            yield
            B.ts(vn[:], vn[:], mv[:, 0:1], mv[:, 2:3], ALU.subtract, ALU.mult, [R("vn"), R("mv"), R("mv4")], [R("vn")])
            B.tt(vn[:], vn[:], lg[:], ALU.mult, [R("vn"), "lg"], [R("vn")], eng="pool")
            B.tt(vnb[:], vn[:], lb[:], ALU.add, [R("vn"), "lb"], [R("vnb")], eng="pool")
            yield
            psm, rm = B.bank()
            for g in range(4):
                B.mm(psm[:, g * 128:(g + 1) * 128], wsT[:, g, :], vnb[:, g * 128:(g + 1) * 128], True, True, ["wsT", R("vnb")], [rm])
            for g in range(4):
                B.stt(ybf[:, g * 128:(g + 1) * 128], psm[:, g * 128:(g + 1) * 128], bT[:, g:g + 1], u[:, g * 128:(g + 1) * 128],
                      ALU.add, ALU.mult, [rm, "bT", R("u")], [R("sybf")])
            yield
            pt, rt = B.bank()
            ptb = pt[:].bitcast(BF16)
            for c4 in range(4):
                B.tr(ptb[:, c4 * 128:(c4 + 1) * 128], ybf[:, c4 * 128:(c4 + 1) * 128], self.identb[:], [R("sybf"), "identb"], [rt])
            B.cp(self.yTs[:, :, slot * 128:(slot + 1) * 128], ptb[:, 0:512].rearrange("p (a t) -> p a t", a=4), [rt],
                 ["yTs%d" % slot])

        B.pipeline([tile(n, s_) for n, s_ in enumerate(slots)], NBUF)
        B.end()

    def stage_merge1(self, do_ctx):
        B, d, l1 = self, self.d, do_ctx
        B.begin()
        wg = [B.sb("wg%d" % i, [128, KC, 3, 128], BF16) for i in range(2)]
        wb = [B.sb("wbr%d" % i, [128, 4, 3, 128], BF16) for i in range(2)]
        sg = [B.sb("sg%d" % i, [128, 512]) for i in range(3)]
        t1 = B.sb("t1", [128, 512])
        t2 = B.sb("t2", [128, 512])
        groups = []
        if l1:
            groups.append((0, 2))
        for g in range(4):
            groups.append((NCT + 4 * g, 4))
        yTs = [self.yTa, self.yTg, self.yTs]
        win = d["w_in"].rearrange("(k p) c -> p k c", p=128)
        for j in range(KC):
            g_ = wg[j % 2]
            b_ = wb[j % 2]
            gr = "wg%d" % (j % 2)
            br = "wbr%d" % (j % 2)
            for b3, c0 in enumerate((MA, MG, MS)):
                B.dma(g_[:, :, b3, :], win[:, :, c0 + j * 128:c0 + (j + 1) * 128], [], [gr + "_%d" % b3], eng="pool")
                B.dma(b_[:, :, b3, :], d["w_br"][b3].rearrange("(k p) c -> p k c", p=128)[:, :, j * 128:(j + 1) * 128], [], [br + "_%d" % b3], eng="pool")
            for (slot0, nsl) in groups:
                ntok = nsl * 128
                tok0 = slot0 * 128
                hres = ["hT%d" % s for s in range(slot0, slot0 + nsl)]
                pgs = []
                for b3 in range(3):
                    ps, rp = B.bank()
                    for k in range(KC):
                        B.mm(ps[:, 0:ntok], g_[:, k, b3, :], self.hT[:, k, tok0:tok0 + ntok], k == 0, k == KC - 1, [gr + "_%d" % b3], [rp])
                    B.act(sg[b3][:, 0:ntok], ps[:, 0:ntok], AF.Sigmoid, [rp], ["sg%d" % b3])
                pbs = []
                for b3 in range(3):
                    ps, rp = B.bank()
                    for k in range(4):
                        B.mm(ps[:, 0:ntok], b_[:, k, b3, :], yTs[b3][:, k, tok0:tok0 + ntok], k == 0, k == 3, [br + "_%d" % b3], [rp])
                    pbs.append((ps, rp))
                B.tt(t1[:, 0:ntok], pbs[0][0][:, 0:ntok], sg[0][:, 0:ntok], ALU.mult, [pbs[0][1], "sg0"], ["t1"])
                B.tt(t2[:, 0:ntok], pbs[1][0][:, 0:ntok], sg[1][:, 0:ntok], ALU.mult, [pbs[1][1], "sg1"], ["t2"])
                B.tt(t1[:, 0:ntok], t1[:, 0:ntok], t2[:, 0:ntok], ALU.add, ["t1", "t2"], ["t1"], eng="pool")
                B.tt(t2[:, 0:ntok], pbs[2][0][:, 0:ntok], sg[2][:, 0:ntok], ALU.mult, [pbs[2][1], "sg2", "t2"], ["t2"])
                B.tt(self.mT[:, j, tok0:tok0 + ntok], t1[:, 0:ntok], t2[:, 0:ntok], ALU.add, ["t1", "t2"], ["mT"], eng="pool")
        B.end()

    def stage_merge2(self, c, do_ctx):
        B, d, l1 = self, self.d, do_ctx
        dc = self.dcom
        B.begin()
        wo = B.sb("wo", [128, KC, D], BF16)
        B.dma(wo[:], d["w_out"].rearrange("(k p) c -> p k c", p=128), [], ["wo"], eng="pool")
        pg = B.sb("pg", [128, D])
        pb = B.sb("pb", [128, D])
        B.dma(pg[:], d["post_g"].broadcast_to([128, D]), [], ["pg"])
        B.dma(pb[:], d["post_b"].broadcast_to([128, D]), [], ["pb"])
        NBUF = 4
        sets = [dict(i=i, x=B.sb("mxt", [128, D]), r=B.sb("rr", [128, D]), st6=B.sb("mst6", [128, 2, 6]), mv=B.sb("mmv", [128, 4]))
                for i in range(NBUF)]
        slots = list(range(NS)) if l1 else list(range(NCT, NS))

        def tile(n, slot):
            bs = sets[n % NBUF]
            R = lambda nm: "%s%d" % (nm, bs["i"])
            x, r, st6, mv = bs["x"], bs["r"], bs["st6"], bs["mv"]
            which = 1 if slot < NCT else 0
            B.dma(x[:], self.src_rows(slot), [], [R("mxt")])
            for half in range(2):
                ps, rp = B.bank()
                for k in range(KC):
                    B.mm(ps[:, :], self.mT[:, k, slot * 128:(slot + 1) * 128], wo[:, k, half * 512:(half + 1) * 512], k == 0, k == KC - 1,
                         ["wo"], [rp])
                hs = slice(half * 512, (half + 1) * 512)
                hres = R("rr") + "h%d" % half
                B.tt(r[:, hs], ps[:, :], self.gateb[:, which, hs], ALU.mult, [rp, "gateb"], [hres])
                B.stt(r[:, hs], x[:, hs], ALPHA, r[:, hs], ALU.mult, ALU.add, [R("mxt"), hres], [hres])
                B.S.add("dve", lambda e, r=r, hs=hs, half=half: e.bn_stats(out=st6[:, half, :], in_=r[:, hs]),
                        reads=[hres], writes=[R("mst6_%d" % half)])
            yield
            B.S.add("dve", lambda e: e.bn_aggr(out=mv[:, 0:2], in_=st6[:].rearrange("p a b -> p (a b)")),
                    reads=[R("mst6_0"), R("mst6_1")], writes=[R("mmv")])
            B.ts(mv[:, 2:3], mv[:, 1:2], EPS, None, ALU.add, None, [R("mmv")], [R("mmv2")])
            B.act(mv[:, 3:4], mv[:, 2:3], AF.Ln, [R("mmv2")], [R("mmv3")])
            B.act(mv[:, 2:3], mv[:, 3:4], AF.Exp, [R("mmv3"), R("mmv2")], [R("mmv4")], scale=-0.5)
            yield
            rres = R("rr")
            B.ts(r[:], r[:], mv[:, 0:1], mv[:, 2:3], ALU.subtract, ALU.mult, [rres + "h0", rres + "h1", R("mmv"), R("mmv4")], [rres])
            B.tt(r[:], r[:], pg[:], ALU.mult, [rres, "pg"], [rres], eng="pool")
            B.tt(r[:], r[:], pb[:], ALU.add, [rres, "pb"], [rres], eng="pool")
            t = slot - NCT
            if slot < NCT:
                dst = dc["ctx1s"][slot * 128:(slot + 1) * 128, :]
            elif self.L == 0:
                dst = dc["x1s"][c, t * 128:(t + 1) * 128, :]
            else:
                dst = dc["x_out"][t * 128:(t + 1) * 128, :]
            B.dma(dst, r[:], [rres], [rres + "h0", rres + "h1"], eng="pool")

        B.pipeline([tile(n, s_) for n, s_ in enumerate(slots)], NBUF)
        B.end()


_NC_CACHE = {}


def _get_nc():
    if "f" not in _NC_CACHE:
        _NC_CACHE["f"] = Builder().build()
    return _NC_CACHE["f"]


def _rope_tables(tok0):
    t = np.arange(tok0, tok0 + NT * 128)
    row = (t // 64).astype(np.float32)
    col = (t % 64).astype(np.float32)
    freqs = (np.float32(10000.0) ** (-np.arange(0, 32, 2, dtype=np.float32) / np.float32(32))).astype(np.float32)
    ar = (row[:, None] * freqs[None, :]).astype(np.float32)
    ac = (col[:, None] * freqs[None, :]).astype(np.float32)
    cr, sr, cc, sc = np.cos(ar), np.sin(ar), np.cos(ac), np.sin(ac)
    cosF = np.concatenate([cr, cr, cc, cc], axis=1).astype(np.float32)
    sinF = np.concatenate([-sr, sr, -sc, sc], axis=1).astype(np.float32)

    def lay(a):
        return np.ascontiguousarray(a.reshape(NT, 128, 64).transpose(1, 0, 2))
    return lay(cosF), lay(sinF)


def _consts():
    c = np.zeros((128, 3, 128), np.float32)
    c[:, 0, :] = np.eye(128, dtype=np.float32)
    c[:, 1, :] = np.triu(np.ones((128, 128), np.float32))
    c[:, 2, :] = np.tril(np.ones((128, 128), np.float32))
    return c


_HPERM = [0, 4, 1, 5, 2, 6, 3, 7]


def _layout_w_in(w):
    o = np.zeros((D, WEND), np.float32)
    src = 0

    def take(n):
        nonlocal src
        s = w[:, src:src + n]
        src += n
        return s
    aq = take(512).reshape(D, 8, 64)[:, _HPERM, :].reshape(D, 512)
    ak = take(128)
    av = take(128)
    ag = take(512).reshape(D, 8, 64)[:, _HPERM, :].reshape(D, 512)
    o[:, AQ:AQ + 512] = aq
    o[:, AK:AK + 128] = ak
    o[:, AV:AV + 128] = av
    o[:, AG:AG + 512] = ag
    o[:, GQ:GQ + 256] = take(256)
    o[:, GK:GK + 256] = take(256)
    o[:, GV:GV + 512] = take(512)
    o[:, GA:GA + 16] = take(16)
    o[:, GA + 32:GA + 48] = take(16)
    o[:, GG:GG + 512] = take(512)
    o[:, SU:SU + 512] = take(512)
    o[:, SV:SV + 512] = take(512)
    o[:, SG:SG + 512] = take(512)
    o[:, MA:MA + 1024] = take(1024)
    o[:, MG:MG + 1024] = take(1024)
    o[:, MS:MS + 1024] = take(1024)
    assert src == 7456
    return o


def kernel(x, c, ctx, c_ctx, w_ada, b_ada, w_in, attn_q_norm, attn_k_norm,
           gla_a2_f, gla_ab_f, gla_a2_b, gla_ab_b, gla_o_norm,
           sgu_ln_g, sgu_ln_b, sgu_w, sgu_b,
           w_br_attn, w_br_gla, w_br_sgu, w_out, post_ln_g, post_ln_b):
    f = lambda a: np.ascontiguousarray(np.asarray(a, dtype=np.float32))
    x = f(x)
    ctx = f(ctx)
    c = f(c)
    c_ctx = f(c_ctx)
    ncores = 8
    consts = _consts()
    ropes = [_rope_tables(j * NT * 128) for j in range(4)]
    ropec_all = np.ascontiguousarray(np.concatenate([r[0] for r in ropes], axis=1))
    ropes_all = np.ascontiguousarray(np.concatenate([r[1] for r in ropes], axis=1))
    shared = dict(consts=consts, ropec_all=ropec_all, ropes_all=ropes_all)
    for l in range(2):
        sfx = "_%d" % l
        a2b = np.zeros((64, 256), np.float32)
        a2b[0:16] = f(gla_a2_f[l])
        a2b[16] = f(gla_ab_f[l])
        a2b[32:48] = f(gla_a2_b[l])
        a2b[48] = f(gla_ab_b[l])
        bl = f(b_ada[l])
        lay = dict(
            w_ada=f(w_ada[l]), b_ada_l=np.ascontiguousarray(bl[0:2048].reshape(16, 128).T),
            b_gate=np.ascontiguousarray(bl[2048:3072].reshape(1, D)), w_in=_layout_w_in(f(w_in[l])),
            k_norm=f(attn_k_norm[l]).reshape(1, 64), q_norm=f(attn_q_norm[l]).reshape(1, 64), a2b=a2b,
            o_norm=f(gla_o_norm[l]).reshape(1, 128), sgu_g=f(sgu_ln_g[l]).reshape(1, 512), sgu_b=f(sgu_ln_b[l]).reshape(1, 512),
            sgu_wT=np.ascontiguousarray(f(sgu_w[l]).transpose(0, 2, 1)), sgu_bT=np.ascontiguousarray(f(sgu_b[l]).T),
            w_br=np.stack([f(w_br_attn[l]).reshape(8, 64, D)[_HPERM].reshape(512, D), f(w_br_gla[l]), f(w_br_sgu[l])], axis=0),
            w_out=f(w_out[l]), post_g=f(post_ln_g[l]).reshape(1, D), post_b=f(post_ln_b[l]).reshape(1, D))
        for k, v in lay.items():
            shared[k + sfx] = v
    maps = []
    for cid in range(ncores):
        b, j = cid // 4, cid % 4
        cv = np.stack([c[b], c_ctx], axis=1)
        m = dict(shared)
        m.update(x=np.ascontiguousarray(x[b].reshape(4, NT * 128, D)), ctx=np.ascontiguousarray(ctx[b]),
                 cvec=np.ascontiguousarray(cv.reshape(KC, 128, 2).transpose(1, 0, 2)),
                 ropec_own=ropes[j][0], ropes_own=ropes[j][1])
        maps.append(m)
    res = run_bass_kernel_spmd(_get_nc(), maps, core_ids=list(range(ncores))).results
    out = np.zeros((2, 8192, D), np.float32)
    for cid in range(ncores):
        out[cid // 4, (cid % 4) * 2048:(cid % 4 + 1) * 2048, :] = np.asarray(res[cid]["x_out"], np.float32)
    return out
```

```python
import contextlib
import numpy as np
import ml_dtypes
import concourse.bass as bass
import concourse.mybir as mybir
from concourse.bass_utils import run_bass_kernel_spmd

F32 = mybir.dt.float32
BF16 = mybir.dt.bfloat16
AF = mybir.ActivationFunctionType
ALU = mybir.AluOpType
AX = mybir.AxisListType

D = 1024
KC = 8
NT = 16
NCT = 2
NS = NT + NCT
TOK = NS * 128
NKT = 2 + 64
AQ, AK, AV, AG, GQ, GK, GV, GA, GG, SU, SV, SG, MA, MG, MS, WEND = (
    0, 512, 640, 768, 1280, 1536, 1792, 2304, 2368, 2880, 3392, 3904, 4416, 5440, 6464, 7488)
ALPHA = 4 ** 0.25
EPS = 1e-6


class _Op:
    __slots__ = ("eng", "fn", "deps", "is_dma", "marked", "tok", "slot_wait", "raw")


class Prog:
    ENGS = ("pe", "act", "dve", "pool", "sp")
    NDMA = 8

    def __init__(self, nc, st):
        self.nc = nc
        self.sems = {}
        for e in self.ENGS:
            self.sems[("c", e, 0)] = st.enter_context(nc.semaphore("c_" + e))
            if e in ("sp", "pool", "act"):
                for s in range(self.NDMA):
                    self.sems[("d", e, s)] = st.enter_context(nc.semaphore("d_%s%d" % (e, s)))
        self.ccount = {e: 0 for e in self.ENGS}
        self.ndma = {e: 0 for e in self.ENGS}


class Sched:
    def __init__(self, prog):
        self.p = prog
        self.ops = {e: [] for e in Prog.ENGS}
        self.last_w = {}
        self.readers = {}

    def add(self, eng, fn, reads=(), writes=(), dma=False, raw=False):
        op = _Op()
        op.raw = raw
        op.eng, op.fn, op.is_dma, op.marked = eng, fn, dma, False
        op.tok = None
        op.slot_wait = None
        deps = []
        for r in reads:
            lw = self.last_w.get(r)
            if lw is not None:
                deps.append(lw)
        for w in writes:
            lw = self.last_w.get(w)
            if lw is not None:
                deps.append(lw)
            deps.extend(self.readers.get(w, ()))
            self.readers[w] = []
            self.last_w[w] = op
        for r in reads:
            self.readers.setdefault(r, []).append(op)
        dd = []
        seen = set()
        for d in deps:
            if d is op or id(d) in seen:
                continue
            seen.add(id(d))
            if (not d.is_dma) and d.eng == "pe" and eng == "pe" and not dma:
                continue
            d.marked = True
            dd.append(d)
        op.deps = dd
        if dma:
            n = self.p.ndma[eng]
            self.p.ndma[eng] = n + 1
            K = Prog.NDMA
            op.tok = ("d", eng, n % K, 16 * (n // K + 1))
            op.slot_wait = 16 * (n // K)
        self.ops[eng].append(op)
        return op

    def emit(self):
        p = self.p
        nc = p.nc
        for e in Prog.ENGS:
            c = p.ccount[e]
            for op in self.ops[e]:
                if op.is_dma:
                    continue
                if op.marked:
                    c += 1
                    op.tok = ("c", e, 0, c)
            p.ccount[e] = c
        sems = p.sems
        K = Prog.NDMA
        with nc.Block() as block:
            dec = {"pe": block.tensor, "act": block.scalar, "dve": block.vector,
                   "pool": block.gpsimd, "sp": block.sync}
            for e in Prog.ENGS:
                ops = self.ops[e]
                if not ops:
                    continue

                def body(engine, ops=ops, e=e):
                    waited = {}
                    for op in ops:
                        need = {}
                        for d in op.deps:
                            k = d.tok[:3]
                            v = d.tok[3]
                            if need.get(k, 0) < v:
                                need[k] = v
                        if op.is_dma and op.slot_wait:
                            k = op.tok[:3]
                            if need.get(k, 0) < op.slot_wait:
                                need[k] = op.slot_wait
                        for k, v in need.items():
                            if waited.get(k, 0) >= v:
                                continue
                            waited[k] = v
                            engine.wait_ge(sems[k], v)
                        ins = op.fn(engine)
                        if op.raw:
                            continue
                        if op.is_dma:
                            ins.then_inc(sems[op.tok[:3]], 16)
                        elif op.marked:
                            ins.then_inc(sems[op.tok[:3]], 1)
                    n = p.ndma[e]
                    if n:
                        for s in range(K):
                            cnt = (n - s + K - 1) // K if n > s else 0
                            if cnt:
                                engine.wait_ge(sems[("d", e, s)], 16 * cnt)

                dec[e](body)


class Builder:
    def __init__(self):
        self.phase = "B"
        self.layer1 = True
        self.dyn = False
        self.jv = None
        self.nc = bass.Bass("TRN2", target_bir_lowering=False)
        self.gst = contextlib.ExitStack()
        self.prog = Prog(self.nc, self.gst)
        self.banks = [self.gst.enter_context(self.nc.psum_tensor("bank%d" % i, [128, 512], F32))
                      for i in range(8)]
        self.bi = 0
        self.S = None
        self.sst = None
        self.uid = 0

    def din(self, name, shape, dt=F32):
        return self.nc.dram_tensor(name, list(shape), dt, kind="ExternalInput").ap()

    def dout(self, name, shape, dt=F32):
        return self.nc.dram_tensor(name, list(shape), dt, kind="ExternalOutput").ap()

    def begin(self):
        self.S = Sched(self.prog)
        self.sst = contextlib.ExitStack()
        self.set_banks(range(8))
        if self.dyn:
            self.S.add("sp", self._load_jv, raw=True)

    def end(self):
        self.S.emit()
        self.sst.close()
        self.S = None

    def sb(self, name, shape, dt=F32, persistent=False):
        st = self.gst if persistent else self.sst
        self.uid += 1
        return st.enter_context(self.nc.sbuf_tensor("sb%d_%s" % (self.uid, name), list(shape), dt))

    def set_banks(self, lst):
        self.bank_list = list(lst)
        self.bi = 0

    def bank(self):
        i = self.bank_list[self.bi % len(self.bank_list)]
        self.bi += 1
        return self.banks[i], "bank%d" % i

    def rbank(self, i):
        return self.banks[i], "bank%d" % i

    def mm(self, out, lhsT, rhs, start, stop, r, w):
        self.S.add("pe", lambda e: e.matmul(out, lhsT=lhsT, rhs=rhs, start=start, stop=stop), reads=r, writes=w)

    def tr(self, out, in_, ident, r, w):
        self.S.add("pe", lambda e: e.transpose(out, in_=in_, identity=ident), reads=r, writes=w)

    def act(self, out, in_, func, r, w, bias=None, scale=None):
        kw = {}
        if bias is not None:
            kw["bias"] = bias
        if scale is not None:
            kw["scale"] = scale
        self.S.add("act", lambda e: e.activation(out=out, in_=in_, func=func, **kw), reads=r, writes=w)

    def tt(self, out, in0, in1, op, r, w, eng="dve"):
        self.S.add(eng, lambda e: e.tensor_tensor(out=out, in0=in0, in1=in1, op=op), reads=r, writes=w)

    def ts(self, out, in0, s1, s2, op0, op1, r, w, eng="dve"):
        if op1 is None:
            self.S.add(eng, lambda e: e.tensor_scalar(out=out, in0=in0, scalar1=s1, scalar2=None, op0=op0), reads=r, writes=w)
        else:
            self.S.add(eng, lambda e: e.tensor_scalar(out=out, in0=in0, scalar1=s1, scalar2=s2, op0=op0, op1=op1), reads=r, writes=w)

    def stt(self, out, in0, scalar, in1, op0, op1, r, w):
        self.S.add("dve", lambda e: e.scalar_tensor_tensor(out=out, in0=in0, scalar=scalar, in1=in1, op0=op0, op1=op1), reads=r, writes=w)

    def cp(self, out, in_, r, w, eng="dve"):
        if eng == "act":
            self.S.add("act", lambda e: e.copy(out=out, in_=in_), reads=r, writes=w)
        else:
            self.S.add(eng, lambda e: e.tensor_copy(out=out, in_=in_), reads=r, writes=w)

    def memset(self, ap, val, w, eng="dve"):
        self.S.add(eng, lambda e: e.memset(ap, val), writes=w)

    def dma(self, out, in_, r, w, eng="sp"):
        def fn(e):
            o = out() if callable(out) else out
            i = in_() if callable(in_) else in_
            return e.dma_start(out=o, in_=i)
        self.S.add(eng, fn, reads=r, writes=w, dma=True)

    def _load_jv(self, e):
        if self.jv is None:
            self.jv = e.partition_id() % 4
        return None

    def chunk_rows(self, base, c, r0, r1):
        if c == "dyn":
            return lambda: base[bass.ds(self.jv, 1), r0:r1, :].rearrange("o p c -> p (o c)")
        return base[c, r0:r1, :]

    def declare(self):
        I32 = mybir.dt.int32
        dc = {}
        dc["x"] = self.din("x", [4, NT * 128, D])
        dc["ctx"] = self.din("ctx", [NCT * 128, D])
        dc["cvec"] = self.din("cvec", [128, KC, 2])
        dc["consts"] = self.din("consts", [128, 3, 128])
        dc["ropec_all"] = self.din("ropec_all", [128, 4 * NT, 64])
        dc["ropes_all"] = self.din("ropes_all", [128, 4 * NT, 64])
        dc["ropec_own"] = self.din("ropec_own", [128, NT, 64])
        dc["ropes_own"] = self.din("ropes_own", [128, NT, 64])
        dc["x_out"] = self.dout("x_out", [NT * 128, D])
        nc = self.nc
        dc["kvk"] = nc.dram_tensor("kvk", [128, 64 * 128], BF16, kind="Internal").ap()
        dc["kvv"] = nc.dram_tensor("kvv", [128, 64, 192], BF16, kind="Internal").ap()
        dc["gstate"] = nc.dram_tensor("gstate", [4, 128, 2, 2, 256], F32, kind="Internal").ap()
        dc["x1s"] = nc.dram_tensor("x1s", [4, NT * 128, D], F32, kind="Internal").ap()
        dc["ctx1s"] = nc.dram_tensor("ctx1s", [NCT * 128, D], F32, kind="Internal").ap()
        dc["x1own"] = nc.dram_tensor("x1own", [1, NT * 128, D], F32, kind="Internal").ap()
        dc["gown"] = nc.dram_tensor("gown", [1, 128, 2, 2, 256], F32, kind="Internal").ap()
        dc["sfall"] = nc.dram_tensor("sfall", [4, NT, 128, 2, 256], F32, kind="Internal").ap()
        dc["kvb"] = nc.dram_tensor("kvb", [NCT + 4 * NT, 128, 2, 257], F32, kind="Internal").ap()
        dc["sfown"] = nc.dram_tensor("sfown", [1, NT, 128, 2, 256], F32, kind="Internal").ap()
        self.dcom = dc
        self.dl = []
        for l in range(2):
            sfx = "_%d" % l
            d = {}
            for name, shape in (("w_ada", [D, 3 * D]), ("b_ada_l", [128, 16]), ("b_gate", [1, D]), ("w_in", [D, WEND]),
                                ("k_norm", [1, 64]), ("q_norm", [1, 64]), ("a2b", [64, 256]), ("o_norm", [1, 128]),
                                ("sgu_g", [1, 512]), ("sgu_b", [1, 512]), ("sgu_wT", [4, 128, 128]), ("sgu_bT", [128, 4]),
                                ("w_br", [3, 512, D]), ("w_out", [D, D]), ("post_g", [1, D]), ("post_b", [1, D])):
                d[name] = self.din(name + sfx, shape)
            d["cvec"] = dc["cvec"]
            d["consts"] = dc["consts"]
            d["kt_all"] = dc["kvk"]
            d["v_all"] = dc["kvv"]
            self.dl.append(d)

    def build(self):
        self.declare()
        B = self
        self.cst = B.sb("cst", [128, 3, 128], F32, True)
        self.identb = B.sb("identb", [128, 128], BF16, True)
        self.ones1 = B.sb("ones1", [128, 1], F32, True)
        self.mod = B.sb("mod", [128, 16, 2], F32, True)
        self.gateb = B.sb("gateb", [128, 2, D], F32, True)
        self.a2b = B.sb("a2bt", [64, 256], F32, True)
        dc = self.dcom
        for l in range(2):
            self.L = l
            self.d = self.dl[l]
            self.layer1 = (l == 0)
            self.xsrc = dc["x"] if l == 0 else dc["x1s"]
            self.csrc = dc["ctx"] if l == 0 else dc["ctx1s"]
            self.dyn = False
            self.gsrc = dc["gstate"]
            self.sfsrc = dc["sfall"]
            self.stage_ada()
            self.phaseA()
            if l == 0:
                for c in range(4):
                    self.phaseB(c, do_ctx=(c == 0))
            else:
                self.stage_dynsel()
                self.xsrc = dc["x1own"]
                self.gsrc = dc["gown"]
                self.sfsrc = dc["sfown"]
                self.phaseB(0, do_ctx=False, rope_c="dyn")
        self.gst.close()
        return self.nc

    def phaseA(self):
        B = self
        scope = contextlib.ExitStack()
        self.hT = scope.enter_context(self.nc.sbuf_tensor("sb_hTall_%d" % self.L, [128, KC, (NCT + 4 * NT) * 128], BF16))
        slots = list(range(NCT + 4 * NT))

        def src(slot):
            if slot < NCT:
                return self.csrc[slot * 128:(slot + 1) * 128, :]
            t = slot - NCT
            return self.xsrc[t // NT, (t % NT) * 128:(t % NT + 1) * 128, :]
        self.stage_hT(slots, src)
        for c in range(4):
            self.stage_attn_kv(c)
        self.stage_gla("A", None, False)
        scope.close()

    def stage_dynsel(self):
        B, dc = self, self.dcom
        self.dyn = True
        B.begin()
        for q in range(4):
            r0, r1 = q * 512, (q + 1) * 512
            B.dma(dc["x1own"][0, r0:r1, :], self.chunk_rows(dc["x1s"], "dyn", r0, r1), [], [])
        B.dma(dc["gown"][0], lambda: dc["gstate"][bass.ds(self.jv, 1), :, :, :, :].rearrange("o p a b c -> p (o a) b c"), [], [])
        for q in range(4):
            B.dma(dc["sfown"][0, q * 4:(q + 1) * 4].rearrange("t p a b -> p t a b"),
                  lambda q=q: dc["sfall"][bass.ds(self.jv, 1), q * 4:(q + 1) * 4, :, :, :].rearrange("o t p a b -> p (o t) a b"), [], [])
        B.end()
        self.dyn = False

    def phaseB(self, c, do_ctx, rope_c=None):
        B = self
        if rope_c is None:
            rope_c = c
        tag = "%d_%s" % (self.L, c)
        self.main = contextlib.ExitStack()
        self.hT = self.main.enter_context(self.nc.sbuf_tensor("sb_hT_" + tag, [128, KC, TOK], BF16))
        self.yTa = self.main.enter_context(self.nc.sbuf_tensor("sb_yTa_" + tag, [128, 4, TOK], BF16))

        def src(slot):
            if slot < NCT:
                return self.csrc[slot * 128:(slot + 1) * 128, :]
            t = slot - NCT
            return self.chunk_rows(self.xsrc, c, t * 128, (t + 1) * 128)
        self.src_rows = src
        self.stage_attn(rope_c, do_ctx, hT_src=src)
        self.yTg = self.main.enter_context(self.nc.sbuf_tensor("sb_yTg_" + tag, [128, 4, TOK], BF16))
        self.stage_gla("B", c, do_ctx)
        self.yTs = self.main.enter_context(self.nc.sbuf_tensor("sb_yTs_" + tag, [128, 4, TOK], BF16))
        self.stage_sgu(do_ctx)
        self.mT = self.main.enter_context(self.nc.sbuf_tensor("sb_mT_" + tag, [128, KC, TOK], BF16))
        self.stage_merge1(do_ctx)
        self.stage_merge2(c, do_ctx)
        self.main.close()

    def stage_ada(self):
        B, d = self, self.d
        need_gate = True
        B.begin()
        cv = B.sb("cv", [128, KC, 2])
        sc = B.sb("sc", [128, KC, 2])
        wk = [B.sb("wk%d" % i, [128, 3 * D]) for i in range(2)]
        bada = B.sb("bada", [128, 16])
        B.dma(self.cst[:], d["consts"], [], ["cst"])
        B.dma(self.a2b[:], d["a2b"], [], ["a2b"])
        B.cp(self.identb[:], self.cst[:, 0, :], ["cst"], ["identb"])
        B.memset(self.ones1[:], 1.0, ["ones1"])
        B.dma(cv[:], d["cvec"], [], ["cv"])
        B.dma(bada[:], d["b_ada_l"], [], ["bada"])
        B.act(sc[:], cv[:], AF.Silu, ["cv"], ["sc"])
        psA, rA = B.bank()
        psAv = psA[:, 0:32].rearrange("p (o t) -> p o t", t=2)
        if need_gate:
            scb = B.sb("scb", [128, KC, 2, 128])
            bg = B.sb("bg", [128, D])
            B.dma(bg[:], d["b_gate"].broadcast_to([128, D]), [], ["bg"])
            B.cp(scb[:], sc[:].unsqueeze(3).broadcast_to([128, KC, 2, 128]), ["sc"], ["scb"])
            psG = [B.bank() for _ in range(4)]
        for k in range(KC):
            w = wk[k % 2]
            wr = "wk%d" % (k % 2)
            B.dma(w[:], d["w_ada"][k * 128:(k + 1) * 128, :], [], [wr])
            for oc in range(16):
                B.mm(psAv[:, oc, :], w[:, oc * 128:(oc + 1) * 128], sc[:, k, :], k == 0 and oc == 0, k == KC - 1 and oc == 15,
                     [wr, "sc"], [rA])
            if need_gate:
                for which in range(2):
                    for half in range(2):
                        pg, rg = psG[which * 2 + half]
                        B.mm(pg[:, :], scb[:, k, which, :], w[:, 2048 + half * 512:2048 + (half + 1) * 512],
                             k == 0, k == KC - 1, [wr, "scb"], [rg])
        B.tt(self.mod[:], psAv, bada[:].unsqueeze(2).broadcast_to([128, 16, 2]), ALU.add, [rA, "bada"], ["mod"])
        B.ts(self.mod[:, 8:16, :], self.mod[:, 8:16, :], 1.0, None, ALU.add, None, ["mod"], ["mod"])
        if need_gate:
            for which in range(2):
                for half in range(2):
                    pg, rg = psG[which * 2 + half]
                    B.tt(self.gateb[:, which, half * 512:(half + 1) * 512], pg[:, :], bg[:, half * 512:(half + 1) * 512],
                         ALU.add, [rg, "bg"], ["gateb"])
        B.end()

    def stage_hT(self, slots, src, embedded=False):
        B = self
        if not embedded:
            B.begin()
        xt = [B.sb("xt%d" % i, [128, D]) for i in range(3)]
        for n, slot in enumerate(slots):
            x = xt[n % 3]
            xr = "xt%d" % (n % 3)
            which = 1 if slot < NCT else 0
            B.dma(x[:], src(slot), [], [xr])
            for half in range(2):
                ps, rp = B.bank()
                for j in range(4):
                    k = half * 4 + j
                    B.tr(ps[:, j * 128:(j + 1) * 128], x[:, k * 128:(k + 1) * 128], self.cst[:, 0, :], [xr, "cst"], [rp])
                for j in range(4):
                    k = half * 4 + j
                    if half == 0:
                        B.ts(self.hT[:, k, slot * 128:(slot + 1) * 128], ps[:, j * 128:(j + 1) * 128],
                             self.mod[:, 8 + k, which:which + 1], self.mod[:, k, which:which + 1], ALU.mult, ALU.add,
                             [rp, "mod"], ["hT%d" % slot])
                    else:
                        B.act(self.hT[:, k, slot * 128:(slot + 1) * 128], ps[:, j * 128:(j + 1) * 128], AF.Identity,
                              [rp, "mod"], ["hT%d" % slot], bias=self.mod[:, k, which:which + 1], scale=self.mod[:, 8 + k, which:which + 1])
        if not embedded:
            B.end()

    def loadw(self, dst, c0, n, res):
        src = self.d["w_in"].rearrange("(k p) c -> p k c", p=128)[:, :, c0:c0 + n]
        self.dma(dst, src, [], [res], eng="pool")

    def proj_tm(self, slot, wt, wres, c0, n, ps, rp):
        for k in range(KC):
            self.mm(ps[:, 0:n], self.hT[:, k, slot * 128:(slot + 1) * 128], wt[:, k, c0:c0 + n], k == 0, k == KC - 1,
                    ["hT%d" % slot, wres], [rp])

    def proj_fm(self, slot0, nslots, wt, wres, c0, m, ps, rp):
        ntok = nslots * 128
        for k in range(KC):
            self.mm(ps[0:m, 0:ntok], wt[:, k, c0:c0 + m], self.hT[:, k, slot0 * 128:slot0 * 128 + ntok], k == 0, k == KC - 1,
                    ["hT%d" % s for s in range(slot0, slot0 + nslots)] + [wres], [rp])

    def qk_prep(self, ps, rp, nh, gain_b, rope_idx, out, rout, scr):
        pset = dict(s1=self.s1, s2=self.s2, s3=self.s3, sm=self.sm, sfx="")
        for _ in self.qk_prep_g(ps, rp, nh, gain_b, rope_idx, out, rout, pset):
            pass

    def qk_prep_g(self, ps, rp, nh, gain_b, rope_idx, out, rout, pset):
        B = self
        W = nh * 64
        s1, s2, s3, sm, x = pset["s1"], pset["s2"], pset["s3"], pset["sm"], pset["sfx"]
        R = lambda n: n + x
        B.tt(s1[:, 0:W], ps[:, 0:W], ps[:, 0:W], ALU.mult, [rp], [R("qs1")])
        yield
        B.S.add("dve", lambda e: e.tensor_reduce(out=sm[:, 0:nh], in_=s1[:, 0:W].rearrange("p (h d) -> p h d", h=nh),
                                                axis=AX.X, op=ALU.add), reads=[R("qs1")], writes=[R("qsm")])
        B.ts(sm[:, 8:8 + nh], sm[:, 0:nh], 1.0 / 64.0, EPS, ALU.mult, ALU.add, [R("qsm")], [R("qsm2")])
        B.act(sm[:, 16:16 + nh], sm[:, 8:8 + nh], AF.Ln, [R("qsm2")], [R("qsm3")])
        B.act(sm[:, 24:24 + nh], sm[:, 16:16 + nh], AF.Exp, [R("qsm3")], [R("qsm4")], scale=-0.5)
        yield
        B.tt(s2[:, 0:W].rearrange("p (h d) -> p h d", h=nh), ps[:, 0:W].rearrange("p (h d) -> p h d", h=nh),
             sm[:, 24:24 + nh].unsqueeze(2).broadcast_to([128, nh, 64]), ALU.mult, [rp, R("qsm4")], [R("qs2")])
        if rope_idx is None:
            B.tt(out.rearrange("p (h d) -> p h d", h=nh), s2[:, 0:W].rearrange("p (h d) -> p h d", h=nh),
                 gain_b[:].unsqueeze(1).broadcast_to([128, nh, 64]), ALU.mult, [R("qs2"), "gain"], [rout])
            return
        B.tt(s1[:, 0:W].rearrange("p (h d) -> p h d", h=nh), s2[:, 0:W].rearrange("p (h d) -> p h d", h=nh),
             gain_b[:].unsqueeze(1).broadcast_to([128, nh, 64]), ALU.mult, [R("qs2"), "gain", R("qs1")], [R("qs1")])
        cosF = self.ropec[:, rope_idx, :]
        sinF = self.ropes[:, rope_idx, :]
        B.tt(s2[:, 0:W].rearrange("p (h d) -> p h d", h=nh), s1[:, 0:W].rearrange("p (h d) -> p h d", h=nh),
             cosF.unsqueeze(1).broadcast_to([128, nh, 64]), ALU.mult, [R("qs1"), "rope"], [R("qs2")])
        q5 = s1[:, 0:W].rearrange("p (h a f i) -> p h a f i", h=nh, a=2, f=2)
        t5 = s3[:, 0:W].rearrange("p (h a f i) -> p h a f i", h=nh, a=2, f=2)
        sn = sinF.rearrange("p (a f i) -> p a f i", a=2, f=2)
        for f in range(2):
            B.tt(t5[:, :, :, f, :], q5[:, :, :, 1 - f, :],
                 sn[:, :, f, :].unsqueeze(1).broadcast_to([128, nh, 2, 16]), ALU.mult, [R("qs1"), "rope"], [R("qs3_%d" % f)],
                 eng="pool")
        B.tt(out, s2[:, 0:W], s3[:, 0:W], ALU.add, [R("qs2"), R("qs3_0"), R("qs3_1")], [rout])

    def prep_common(self, rope_c):
        B, d = self, self.d
        dc = self.dcom
        if rope_c == "dyn":
            rc_src, rs_src = dc["ropec_own"], dc["ropes_own"]
        else:
            rc_src = dc["ropec_all"][:, rope_c * NT:(rope_c + 1) * NT, :]
            rs_src = dc["ropes_all"][:, rope_c * NT:(rope_c + 1) * NT, :]
        self.s1 = B.sb("s1", [128, 512])
        self.s2 = B.sb("s2", [128, 512])
        self.s3 = B.sb("s3", [128, 512])
        self.sm = B.sb("sm", [128, 32])
        self.stg = B.sb("stg", [128, 512])
        self.ropec = B.sb("ropec", [128, NT, 64])
        self.ropes = B.sb("ropes", [128, NT, 64])
        self.gk = B.sb("gk", [128, 64])
        B.dma(self.ropec[:], rc_src, [], ["rope"])
        B.dma(self.ropes[:], rs_src, [], ["rope"])
        B.dma(self.gk[:], d["k_norm"].broadcast_to([128, 64]), [], ["gain"])

    def kv_slot(self, slot, wt, wres, KT, ktres, col0, V, vres, vt, ridx):
        B = self
        ps, rp = B.bank()
        B.proj_tm(slot, wt, wres, AK - self.wbase, 256, ps, rp)
        B.cp(self.stg[:, 0:256], ps[:, 0:256], [rp], ["stg"])
        ps, rp = self.stg, "stg"
        kb = self.kbf
        B.qk_prep(ps, rp, 2, self.gk, ridx, kb[:], "kbf", (self.s1, self.s2, self.sm))
        B.cp(V[:, vt, 0:64], ps[:, 128:192], [rp], [vres], eng="act")
        B.cp(V[:, vt, 128:192], ps[:, 192:256], [rp], [vres], eng="act")
        pt, rt = B.bank()
        ptb = pt[:].bitcast(BF16)
        B.tr(ptb[:, 0:128], kb[:], self.identb[:], ["kbf", "identb"], [rt])
        B.cp(KT[:, col0:col0 + 128], ptb[:, 0:128], [rt], [ktres])

    def stage_attn_kv(self, c):
        B, d = self, self.d
        B.begin()
        self.prep_common(c)
        wt = B.sb("wkv", [128, KC, 256], BF16)
        self.wbase = AK
        B.loadw(wt[:], AK, 256, "wkv")
        KT = B.sb("KTo", [128, NT * 128], BF16)
        V = B.sb("Vo", [128, NT, 192], BF16)
        B.memset(V[:], 1.0, ["Vo"])
        NSET = 4
        psets = [dict(s1=B.sb("ks1", [128, 128]), s2=B.sb("ks2", [128, 128]), s3=B.sb("ks3", [128, 128]), sm=B.sb("ksm", [128, 32]),
                      stg=B.sb("kstg", [128, 256]), kb=B.sb("kkb", [128, 128], BF16), sfx="_%d" % i) for i in range(NSET)]

        def slot_gen(i):
            slot = NCT + c * NT + i
            pset = psets[i % NSET]
            x = pset["sfx"]
            stg, kb = pset["stg"], pset["kb"]
            ps, rp = B.bank()
            B.proj_tm(slot, wt, "wkv", 0, 256, ps, rp)
            B.cp(stg[:], ps[:, 0:256], [rp], ["kstg" + x])
            yield
            B.cp(V[:, i, 0:64], stg[:, 128:192], ["kstg" + x, "Vo"], ["Vo%d" % i], eng="act")
            B.cp(V[:, i, 128:192], stg[:, 192:256], ["kstg" + x, "Vo"], ["Vo%d" % i], eng="act")
            for _ in B.qk_prep_g(stg, "kstg" + x, 2, self.gk, i, kb[:], "kkb" + x, pset):
                yield
            yield
            pt, rt = B.bank()
            ptb = pt[:].bitcast(BF16)
            B.tr(ptb[:, 0:128], kb[:], self.identb[:], ["kkb" + x, "identb"], [rt])
            B.cp(KT[:, i * 128:(i + 1) * 128], ptb[:, 0:128], [rt], ["KTo%d" % i])

        B.pipeline([slot_gen(i) for i in range(NT)], NSET)
        B.dma(self.dcom["kvk"][:, c * NT * 128:(c + 1) * NT * 128], KT[:], ["KTo%d" % i for i in range(NT)], [])
        B.dma(self.dcom["kvv"][:, c * NT:(c + 1) * NT, :], V[:], ["Vo"] + ["Vo%d" % i for i in range(NT)], [])
        B.end()

    def stage_attn(self, c, do_ctx, hT_src=None):
        B, d, l1 = self, self.d, do_ctx
        B.begin()
        self.prep_common(c)
        self.kbf = B.sb("kbf", [128, 128], BF16)
        gq = B.sb("gq", [128, 64])
        B.dma(gq[:], d["q_norm"].broadcast_to([128, 64]), [], ["gain"])
        wt = B.sb("watt", [128, KC, 1280], BF16)
        self.wbase = 0
        B.loadw(wt[:], 0, 1280, "watt")
        KT = B.sb("KT", [128, NKT * 128], BF16)
        V = B.sb("V", [128, NKT, 192], BF16)
        B.memset(V[:, 0:2, :], 1.0, ["Vc"])
        B.dma(KT[:, 256:], d["kt_all"], [], ["KTl"], eng="act")
        B.dma(V[:, 2:, :], d["v_all"], [], ["Vl"], eng="act")
        if hT_src is not None:
            self.stage_hT(list(range(NS)), hT_src, embedded=True)
            B.set_banks(range(8))
        for slot in range(NCT):
            B.kv_slot(slot, wt, "watt", KT, "KTc", slot * 128, V, "Vc", slot, None)
        qbf = B.sb("qbf", [128, 512], BF16)
        QT = [B.sb("QT%d" % i, [128, 2, 4, 512], BF16) for i in range(2)]
        for i in range(2):
            B.memset(QT[i][:], 0.0, ["QT%d" % i])
        pT = [B.sb("pT%d" % i, [128, 512], BF16) for i in range(4)]
        tmp = B.sb("atmp", [128, 512])
        rden = B.sb("rden", [128, 512])
        yTa = self.yTa
        groups = []
        if l1:
            groups.append((0, 2, 0, 2))
        for g in range(4):
            groups.append((NCT + 4 * g, 4, 0, NKT))
        pcount = 0
        pocount = 0

        def make_pieces(gi):
            slot0, nsl, k0, k1 = groups[gi]
            ntok = nsl * 128
            tok0 = slot0 * 128
            qt = QT[gi % 2]
            qres = "QT%d" % (gi % 2)
            pcs = []
            for s in range(slot0, slot0 + nsl):
                pcs.append(lambda s=s: q_piece(s, slot0, qt, qres))
            return pcs

        def g_piece(gi, p4, slot0, nsl):
            ntok = nsl * 128
            tok0 = slot0 * 128
            ps, rp = B.bank()
            B.proj_fm(slot0, nsl, wt, "watt", AG + p4 * 128, 128, ps, rp)
            B.act(yTa[:, p4, tok0:tok0 + ntok], ps[:, 0:ntok], AF.Silu, [rp], ["yTa_%d_%d" % (gi, p4)])

        def q_piece(s, slot0, qt, qres):
            if True:
                ps, rp = B.bank()
                B.proj_tm(s, wt, "watt", AQ, 512, ps, rp)
                B.cp(self.stg[:, :], ps[:, :], [rp], ["stg"])
                ps, rp = self.stg, "stg"
                B.qk_prep(ps, rp, 8, gq, (s - NCT) if s >= NCT else None, qbf[:], "qbf", (self.s1, self.s2, self.sm))
                pt, rt = B.bank()
                ptb = pt[:].bitcast(BF16)
                for p4 in range(4):
                    B.tr(ptb[:, p4 * 128:(p4 + 1) * 128], qbf[:, p4 * 128:(p4 + 1) * 128], self.identb[:], ["qbf", "identb"], [rt])
                for hh_ in range(2):
                    B.cp(qt[64 * hh_:64 * hh_ + 64, hh_, :, (s - slot0) * 128:(s - slot0 + 1) * 128],
                         ptb[64 * hh_:64 * hh_ + 64, 0:512].rearrange("p (a t) -> p a t", a=4), [rt], [qres])

        B.set_banks(range(8))
        for gi_, (slot0_, nsl_, _k0, _k1) in enumerate(groups):
            for p4 in range(4):
                g_piece(gi_, p4, slot0_, nsl_)
        for gi, (slot0, nsl, k0, k1) in enumerate(groups):
            B.set_banks(range(8))
            for pc in make_pieces(gi):
                pc()
            ntok = nsl * 128
            tok0 = slot0 * 128
            qt = QT[gi % 2]
            qres = "QT%d" % (gi % 2)
            nxt = []
            for p4 in range(4):
                for hh in range(2):
                    r0 = 64 * hh
                    B.set_banks(range(2, 8))
                    po, ro = B.rbank(pocount % 2)
                    pocount += 1
                    kts = list(range(k0, k1))
                    nk = len(kts)
                    sbanks = {}

                    def qk(i):
                        kt = kts[i]
                        psb, rs = B.bank()
                        sbanks[i] = (psb, rs)
                        B.mm(psb[:, 0:ntok], KT[:, kt * 128:(kt + 1) * 128], qt[:, hh, p4, 0:ntok], True, True,
                             [qres, "KTc" if kt < 2 else "KTl"], [rs])

                    LA = 3
                    for i in range(min(LA, nk)):
                        qk(i)
                    for i in range(nk):
                        kt = kts[i]
                        psb, rs = sbanks.pop(i)
                        pt_ = pT[pcount % 4]
                        pr = "pT%d" % (pcount % 4)
                        pcount += 1
                        B.act(pt_[:, 0:ntok], psb[:, 0:ntok], AF.Exp, [rs], [pr], scale=0.125)
                        if i + LA < nk:
                            qk(i + LA)
                        B.mm(po[:, 0:ntok], V[:, kt, r0:r0 + 128], pt_[:, 0:ntok], i == 0, i == nk - 1,
                             [pr, "Vc" if kt < 2 else "Vl"], [ro])
                    n0, d0 = (0, 64) if hh == 0 else (64, 0)
                    yres = "yTa_%d_%d" % (gi, p4)
                    B.tt(tmp[n0:n0 + 64, 0:ntok], po[n0:n0 + 64, 0:ntok], yTa[n0:n0 + 64, p4, tok0:tok0 + ntok], ALU.mult,
                         [ro, yres], ["atmp"])
                    B.S.add("dve", lambda e, n0=n0, d0=d0, po=po, ntok=ntok: e.reciprocal(out=rden[n0:n0 + 64, 0:ntok], in_=po[d0:d0 + 64, 0:ntok]),
                            reads=[ro], writes=["rden"])
                    B.tt(yTa[n0:n0 + 64, p4, tok0:tok0 + ntok], tmp[n0:n0 + 64, 0:ntok], rden[n0:n0 + 64, 0:ntok], ALU.mult,
                         ["rden", "atmp"], [yres], eng="pool")
                    if nxt:
                        nxt.pop(0)()
            while nxt:
                nxt.pop(0)()
        B.end()

    def pipeline(self, gens, depth):
        active = []
        it = iter(gens)
        pending = next(it, None)
        while pending is not None or active:
            if pending is not None and len(active) < depth:
                active.append(pending)
                pending = next(it, None)
            for g in list(active):
                try:
                    next(g)
                except StopIteration:
                    active.remove(g)

    def stage_gla(self, ph, c, do_ctx):
        B, d, l1 = self, self.d, do_ctx
        B.begin()
        if ph == "A":
            wt = B.sb("wgla", [128, KC, 832], BF16)
            B.loadw(wt[:], GK, 832, "wgla")
            wb = GK
        else:
            wt = B.sb("wgla", [128, KC, 1600], BF16)
            B.loadw(wt[:], GQ, 1600, "wgla")
            wb = GQ
        Lmat = [self.cst[:, 1, :], self.cst[:, 2, :]]
        NBUF = 3
        B.set_banks(range(6))
        outs_needed = (ph == "B")
        sets = []
        for i in range(NBUF):
            bs = {"i": i}
            bs["paT"] = B.sb("paT", [64, 128])
            B.memset(bs["paT"][:], 1.0, ["paT%d" % i])
            bs["kT"] = B.sb("gkT", [128, 2, 128])
            bs["vt"] = B.sb("gv", [128, 512], BF16)
            bs["lz"] = B.sb("lz", [128, 2, 256])
            bs["ez"] = B.sb("ez", [128, 2, 256])
            bs["exh"] = bs["ez"][:].rearrange("p d (a t) -> p d a t", a=2)
            bs["ntot"] = B.sb("ntot", [128, 2, 2, 1])
            bs["ac"] = B.sb("ac", [128, 2, 2, 1])
            bs["KhT"] = B.sb("KhT", [128, 2, 2, 128], BF16)
            bs["Kh"] = B.sb("Kh", [128, 2, 2, 128], BF16)
            if outs_needed:
                bs["qT"] = B.sb("gqT", [128, 2, 128])
                bs["exq"] = B.sb("exq", [128, 2, 128])
                bs["exk"] = B.sb("exk", [128, 2, 128])
                bs["Qt"] = B.sb("Qt", [128, 2, 2, 2, 128], BF16)
                B.memset(bs["Qt"][:], 0.0, ["Qt%d" % i])
                bs["Kt"] = B.sb("Kt", [128, 2, 2, 128], BF16)
                bs["attm"] = B.sb("attm", [128, 8, 128], BF16)
                bs["gsl"] = B.sb("gsl", [128, 512])
            sets.append(bs)
        Sst = B.sb("Sst", [128, 2, 2, 256])
        Sbf = B.sb("Sbf", [128, 2, 2, 256], BF16)
        Atot = B.sb("Atot", [128, 2, 2, 1])
        if outs_needed:
            Sf_all = B.sb("Sf_all", [128, NS, 2, 256], BF16)
            onb = B.sb("onb", [128, 128])
            B.dma(onb[:], d["o_norm"].broadcast_to([128, 128]), [], ["onb"])
            osb = B.sb("osb", [128, 512])
            osq = B.sb("osq", [128, 512])
            osm = B.sb("osm", [128, 16])
            ybf = B.sb("ybf", [128, 512], BF16)
        counter = [0]

        def tile(slot, dirs, out_dirs, state_dirs, pre_state=None, save_sf=False, dump_dirs=()):
            tno = counter[0]
            counter[0] += 1
            bs = sets[tno % NBUF]
            i = bs["i"]
            R = lambda n: "%s%d" % (n, i)
            kT, vt, paT, lz, ez = bs["kT"], bs["vt"], bs["paT"], bs["lz"], bs["ez"]
            psk, rk = B.bank()
            for pr in range(2):
                B.proj_fm(slot, 1, wt, "wgla", GK - wb + pr * 128, 128, psk[:, pr * 128:(pr + 1) * 128], rk)
            B.cp(kT[:], psk[:, 0:256].rearrange("p (a t) -> p a t", a=2), [rk], [R("kT")], eng="act")
            if out_dirs:
                qT = bs["qT"]
                psq, rq = B.bank()
                for pr in range(2):
                    B.proj_fm(slot, 1, wt, "wgla", GQ - wb + pr * 128, 128, psq[:, pr * 128:(pr + 1) * 128], rq)
                B.ts(qT[:], psq[:, 0:256].rearrange("p (a t) -> p a t", a=2), 0.125, None, ALU.mult, None, [rq], [R("qT")])
            psv, rv = B.bank()
            B.proj_tm(slot, wt, "wgla", GV - wb, 512, psv, rv)
            B.cp(vt[:], psv[:, :], [rv], [R("vt")], eng="act")
            psa, ra = B.bank()
            B.proj_fm(slot, 1, wt, "wgla", GA - wb, 64, psa, ra)
            B.cp(paT[0:16, :], psa[0:16, 0:128], [ra], [R("paT")])
            B.cp(paT[32:48, :], psa[32:48, 0:128], [ra], [R("paT")])
            if out_dirs:
                psg, rg = B.bank()
                B.proj_tm(slot, wt, "wgla", GG - wb, 512, psg, rg)
                B.act(bs["gsl"][:], psg[:, :], AF.Silu, [rg], [R("gsl")])
            yield
            for dr in dirs:
                psz, rz = B.bank()
                B.mm(psz[:, 0:256], paT[32 * dr:32 * dr + 17, :], self.a2b[32 * dr:32 * dr + 17, :], True, True,
                     [R("paT"), "a2b"], [rz])
                B.act(ez[:, dr, :], psz[:, 0:256], AF.Exp, [rz], [R("ez%d" % dr)], scale=-1.0)
                B.act(lz[:, dr, :], ez[:, dr, :], AF.Ln, [R("ez%d" % dr)], [R("lz%d" % dr)], bias=1.0)
            yield
            for dr in dirs:
                lr = R("lz%d" % dr)
                psc, rc = B.bank()
                for pr in range(2):
                    B.mm(psc[:, pr * 128:(pr + 1) * 128], lz[:, dr, pr * 128:(pr + 1) * 128], Lmat[dr], True, True, [lr, "cst"], [rc])
                    B.mm(psc[:, 256 + pr:257 + pr], lz[:, dr, pr * 128:(pr + 1) * 128], self.ones1[:], True, True, [lr, "ones1"], [rc])
                if dr in state_dirs:
                    B.act(bs["ntot"][:, dr, :, 0], psc[:, 256:258], AF.Identity, [rc], [R("ntot")], scale=-1.0 / 16.0)
                    B.act(bs["ac"][:, dr, :, 0], psc[:, 256:258], AF.Exp, [rc], [R("ac")], scale=-1.0 / 16.0)
                cum = psc[:, 0:256].rearrange("p (a t) -> p a t", a=2)
                if dr in out_dirs:
                    exq, exk, Qt, Kt, qT = bs["exq"], bs["exk"], bs["Qt"], bs["Kt"], bs["qT"]
                    B.act(exq[:], cum, AF.Exp, [rc], [R("exq")], scale=-1.0 / 16.0)
                    B.act(exk[:], cum, AF.Exp, [rc], [R("exk")], scale=1.0 / 16.0)
                    for hh_ in range(2):
                        ps_ = slice(64 * hh_, 64 * hh_ + 64)
                        B.tt(Qt[ps_, dr, :, hh_, :], qT[ps_, :, :], exq[ps_, :, :], ALU.mult, [R("qT"), R("exq")], [R("Qt")])
                    B.tt(Kt[:, dr, :, :], kT[:], exk[:], ALU.mult, [R("kT"), R("exk")], [R("Kt")])
                if dr in state_dirs:
                    exh = bs["exh"]
                    for pr in range(2):
                        B.act(exh[:, dr, pr, :], psc[:, pr * 128:(pr + 1) * 128], AF.Exp, [rc, R("ntot")], [R("exh%d" % dr), R("ez%d" % dr)], scale=1.0 / 16.0,
                              bias=bs["ntot"][:, dr, pr, :])
                    B.tt(bs["KhT"][:, dr], kT[:], exh[:, dr], ALU.mult, [R("kT"), R("exh%d" % dr), R("ez%d" % dr)], [R("KhT%d" % dr)])
            yield
            for dr in out_dirs:
                pat, rat = B.bank()
                for pr in range(2):
                    for hh in range(2):
                        h = pr * 2 + hh
                        B.mm(pat[:, h * 128:(h + 1) * 128], bs["Kt"][:, dr, pr, :], bs["Qt"][:, dr, pr, hh, :], True, True,
                             [R("Kt"), R("Qt")], [rat])
                B.tt(bs["attm"][:, dr * 4:(dr + 1) * 4, :], pat[:, :].rearrange("p (h t) -> p h t", h=4),
                     Lmat[dr].unsqueeze(1).broadcast_to([128, 4, 128]), ALU.mult, [rat, "cst"], [R("attm%d" % dr)])
            for dr in state_dirs:
                pst, rt = B.bank()
                ptb = pst[:].bitcast(BF16)
                for pr in range(2):
                    B.tr(ptb[:, pr * 128:(pr + 1) * 128], bs["KhT"][:, dr, pr, :], self.identb[:], [R("KhT%d" % dr), "identb"], [rt])
                B.cp(bs["Kh"][:, dr], ptb[:, 0:256].rearrange("p (a t) -> p a t", a=2), [rt], [R("Kh%d" % dr)])
            yield
            if pre_state is not None:
                pre_state()
            if out_dirs:
                pso, rso = B.rbank(6 + tno % 2)
                n_mm = 0
                tot_mm = len(out_dirs) * 8
                for dr in out_dirs:
                    for h in range(4):
                        pr, hh = h // 2, h % 2
                        if dr in state_dirs:
                            srhs = Sbf[:, dr, pr, hh * 128:(hh + 1) * 128]
                            sres = "Sbf%d" % dr
                        else:
                            srhs = Sf_all[:, slot, pr, hh * 128:(hh + 1) * 128]
                            sres = "Sf%d" % slot
                        B.mm(pso[:, h * 128:(h + 1) * 128], bs["attm"][:, dr * 4 + h, :], vt[:, h * 128:(h + 1) * 128],
                             n_mm == 0, False, [R("attm%d" % dr), R("vt")], [rso])
                        n_mm += 1
                        B.mm(pso[:, h * 128:(h + 1) * 128], bs["Qt"][:, dr, pr, hh, :], srhs,
                             False, n_mm == tot_mm - 1, [R("Qt"), sres], [rso])
                        n_mm += 1
            for dr in state_dirs:
                if save_sf:
                    B.cp(Sf_all[:, slot], Sst[:, dr], ["Sst%d" % dr], ["Sf%d" % slot])
                if dr in dump_dirs:
                    k_ = tno % 2
                    kres = "kvst%d" % k_
                    for pr in range(2):
                        pkv, rkv = B.bank()
                        B.mm(pkv[:, 0:256], bs["Kh"][:, dr, pr, :], vt[:, pr * 256:(pr + 1) * 256], True, True, [R("Kh%d" % dr), R("vt")], [rkv])
                        B.cp(kvst[k_][:, pr, 0:256], pkv[:, 0:256], [rkv], [kres])
                    B.cp(kvst[k_][:, :, 256], bs["ac"][:, dr, :, 0], [R("ac")], [kres])
                    B.dma(self.dcom["kvb"][slot], kvst[k_][:], [kres], ["kvb%d" % slot])
                    continue
                for pr in range(2):
                    pkv, rkv = B.bank()
                    B.mm(pkv[:, 0:256], bs["Kh"][:, dr, pr, :], vt[:, pr * 256:(pr + 1) * 256], True, True, [R("Kh%d" % dr), R("vt")], [rkv])
                    B.stt(Sst[:, dr, pr, :], Sst[:, dr, pr, :], bs["ac"][:, dr, pr, :], pkv[:, 0:256], ALU.mult, ALU.add,
                          [rkv, R("ac"), "Sst%d" % dr], ["Sst%d" % dr])
                B.cp(Sbf[:, dr], Sst[:, dr], ["Sst%d" % dr], ["Sbf%d" % dr], eng="pool")
            yield
            if out_dirs:
                B.cp(osb[:], pso[:, :], [rso], ["osb"])
                B.tt(osq[:], osb[:], osb[:], ALU.mult, ["osb"], ["osq"])
                B.S.add("dve", lambda e: e.tensor_reduce(out=osm[:, 0:4], in_=osq[:].rearrange("p (h d) -> p h d", h=4),
                                                        axis=AX.X, op=ALU.add), reads=["osq"], writes=["osm"])
                B.ts(osm[:, 4:8], osm[:, 0:4], 1.0 / 128.0, EPS, ALU.mult, ALU.add, ["osm"], ["osm2"])
                B.act(osm[:, 8:12], osm[:, 4:8], AF.Ln, ["osm2"], ["osm3"])
                B.act(osm[:, 12:16], osm[:, 8:12], AF.Exp, ["osm3"], ["osm4"], scale=-0.5)
                B.tt(osq[:].rearrange("p (h d) -> p h d", h=4), osb[:].rearrange("p (h d) -> p h d", h=4),
                     osm[:, 12:16].unsqueeze(2).broadcast_to([128, 4, 128]), ALU.mult, ["osb", "osm4", "osq"], ["osq"])
                B.tt(osb[:].rearrange("p (h d) -> p h d", h=4), osq[:].rearrange("p (h d) -> p h d", h=4),
                     onb[:].unsqueeze(1).broadcast_to([128, 4, 128]), ALU.mult, ["osq", "onb", "osb"], ["osb"])
                B.tt(ybf[:], osb[:], bs["gsl"][:], ALU.mult, ["osb", R("gsl")], ["ybf"])
                pt, rt2 = B.bank()
                ptb2 = pt[:].bitcast(BF16)
                for c4 in range(4):
                    B.tr(ptb2[:, c4 * 128:(c4 + 1) * 128], ybf[:, c4 * 128:(c4 + 1) * 128], self.identb[:], ["ybf", "identb"], [rt2])
                B.cp(self.yTg[:, :, slot * 128:(slot + 1) * 128], ptb2[:, 0:512].rearrange("p (a t) -> p a t", a=4), [rt2],
                     ["yTg%d" % slot])

        def zero_state(dr):
            B.memset(Sst[:, dr], 0.0, ["Sst%d" % dr])
            B.memset(Sbf[:, dr], 0.0, ["Sbf%d" % dr], eng="pool")

        B.memset(Atot[:], 1.0, ["Atot"])
        gstate = self.gsrc
        ctxs = list(range(NCT))
        if ph == "A":
            zero_state(0)
            zero_state(1)
            sfst = [B.sb("sfst", [128, 2, 256]) for _ in range(2)]
            kvst = [B.sb("kvst", [128, 2, 257]) for _ in range(2)]
            ring = [B.sb("kvring", [128, 2, 257]) for _ in range(3)]
            fwd = [(s_, None) for s_ in ctxs]
            for cc in range(4):
                for i in range(NT):
                    def hook(cc=cc, i=i):
                        k_ = (cc * NT + i) % 2
                        B.cp(sfst[k_][:], Sst[:, 0], ["Sst0"], ["sfst%d" % k_])
                        if i == 0:
                            B.dma(gstate[cc, :, 0, :, :], sfst[k_][:], ["sfst%d" % k_], [])
                        B.dma(self.dcom["sfall"][cc, i], sfst[k_][:], ["sfst%d" % k_], [])
                    fwd.append((NCT + cc * NT + i, hook))
            B.pipeline([tile(sf, [0, 1], [], [0, 1], pre_state=hf, dump_dirs=(1,)) for (sf, hf) in fwd], NBUF)
            order = list(reversed(ctxs)) + [NCT + cc * NT + i for cc in range(3, -1, -1) for i in range(NT - 1, -1, -1)]
            for idx, slot in enumerate(order):
                rb = ring[idx % 3]
                rres = "kvring%d" % (idx % 3)
                B.dma(rb[:], self.dcom["kvb"][slot], ["kvb%d" % slot], [rres])
                if slot >= NCT and (slot - NCT) % NT == NT - 1:
                    cc = (slot - NCT) // NT
                    k_ = cc % 2
                    B.cp(sfst[k_][:], Sst[:, 1], ["Sst1"], ["sfst%d" % k_])
                    B.dma(gstate[cc, :, 1, :, :], sfst[k_][:], ["sfst%d" % k_], [])
                for pr in range(2):
                    B.stt(Sst[:, 1, pr, :], Sst[:, 1, pr, :], rb[:, pr, 256:257], rb[:, pr, 0:256], ALU.mult, ALU.add,
                          [rres, "Sst1"], ["Sst1"])
        else:
            own = list(range(NCT, NS))
            if l1:
                zero_state(0)
                zero_state(1)
                B.pipeline([tile(s_, [0], [], [0], save_sf=True) for s_ in ctxs], NBUF)
                B.pipeline([tile(s_, [0, 1], [0, 1], [1]) for s_ in reversed(ctxs)], NBUF)
            B.dma(Sst[:, 1], gstate[c, :, 1, :, :], [], ["Sst1"])
            B.cp(Sbf[:, 1], Sst[:, 1], ["Sst1"], ["Sbf1"], eng="pool")
            B.dma(Sf_all[:, NCT:NS], self.sfsrc[c].rearrange("t p a b -> p t a b"), [], ["Sf%d" % s_ for s_ in own], eng="pool")
            B.pipeline([tile(s_, [0, 1], [0, 1], [1]) for s_ in reversed(own)], NBUF)
        B.end()

    def stage_sgu(self, do_ctx):
        B, d, l1 = self, self.d, do_ctx
        B.begin()
        wt = B.sb("wsgu", [128, KC, 1536], BF16)
        B.loadw(wt[:], SU, 1536, "wsgu")
        wsT32 = B.sb("wsT32", [128, 4, 128])
        wsT = B.sb("wsT", [128, 4, 128], BF16)
        B.dma(wsT32[:], d["sgu_wT"].rearrange("g s t -> s g t"), [], ["wsT32"])
        B.cp(wsT[:], wsT32[:], ["wsT32"], ["wsT"])
        bT = B.sb("bT", [128, 4])
        B.dma(bT[:], d["sgu_bT"], [], ["bT"])
        lg = B.sb("lg", [128, 512])
        lb = B.sb("lb", [128, 512])
        B.dma(lg[:], d["sgu_g"].broadcast_to([128, 512]), [], ["lg"])
        B.dma(lb[:], d["sgu_b"].broadcast_to([128, 512]), [], ["lb"])
        NBUF = 4
        sets = []
        for i in range(NBUF):
            sets.append(dict(i=i, st6=B.sb("st6", [128, 6]), mv=B.sb("mv", [128, 4]), vn=B.sb("vn", [128, 512]),
                             vnb=B.sb("vnb", [128, 512], BF16), u=B.sb("u", [128, 512]), gs=B.sb("gs", [128, 512]),
                             ybf=B.sb("sybf", [128, 512], BF16)))
        slots = list(range(NS)) if l1 else list(range(NCT, NS))

        def tile(n, slot):
            bs = sets[n % NBUF]
            R = lambda nm: "%s%d" % (nm, bs["i"])
            st6, mv, vn, vnb, u, gs, ybf = bs["st6"], bs["mv"], bs["vn"], bs["vnb"], bs["u"], bs["gs"], bs["ybf"]
            psu, ru = B.bank()
            B.proj_tm(slot, wt, "wsgu", 0, 512, psu, ru)
            psv, rv = B.bank()
            B.proj_tm(slot, wt, "wsgu", 512, 512, psv, rv)
            psg, rg = B.bank()
            B.proj_tm(slot, wt, "wsgu", 1024, 512, psg, rg)
            B.act(gs[:], psg[:, :], AF.Silu, [rg], [R("gs")])
            B.cp(vn[:], psv[:, :], [rv], [R("vn")], eng="act")
            B.tt(u[:], psu[:, :], gs[:], ALU.mult, [ru, R("gs")], [R("u")])
            yield
            B.S.add("dve", lambda e: e.bn_stats(out=st6[:], in_=vn[:]), reads=[R("vn")], writes=[R("st6")])
            B.S.add("dve", lambda e: e.bn_aggr(out=mv[:, 0:2], in_=st6[:]), reads=[R("st6")], writes=[R("mv")])
            B.ts(mv[:, 2:3], mv[:, 1:2], EPS, None, ALU.add, None, [R("mv")], [R("mv2")])
            B.act(mv[:, 3:4], mv[:, 2:3], AF.Ln, [R("mv2")], [R("mv3")])
            B.act(mv[:, 2:3], mv[:, 3:4], AF.Exp, [R("mv3"), R("mv2")], [R("mv4")], scale=-0.5)
            yield
            B.ts(vn[:], vn[:], mv[:, 0:1], mv[:, 2:3], ALU.subtract, ALU.mult, [R("vn"), R("mv"), R("mv4")], [R("vn")])
            B.tt(vn[:], vn[:], lg[:], ALU.mult, [R("vn"), "lg"], [R("vn")], eng="pool")
            B.tt(vnb[:], vn[:], lb[:], ALU.add, [R("vn"), "lb"], [R("vnb")], eng="pool")
            yield
            psm, rm = B.bank()
            for g in range(4):
                B.mm(psm[:, g * 128:(g + 1) * 128], wsT[:, g, :], vnb[:, g * 128:(g + 1) * 128], True, True, ["wsT", R("vnb")], [rm])
            for g in range(4):
                B.stt(ybf[:, g * 128:(g + 1) * 128], psm[:, g * 128:(g + 1) * 128], bT[:, g:g + 1], u[:, g * 128:(g + 1) * 128],
                      ALU.add, ALU.mult, [rm, "bT", R("u")], [R("sybf")])
            yield
            pt, rt = B.bank()
            ptb = pt[:].bitcast(BF16)
            for c4 in range(4):
                B.tr(ptb[:, c4 * 128:(c4 + 1) * 128], ybf[:, c4 * 128:(c4 + 1) * 128], self.identb[:], [R("sybf"), "identb"], [rt])
            B.cp(self.yTs[:, :, slot * 128:(slot + 1) * 128], ptb[:, 0:512].rearrange("p (a t) -> p a t", a=4), [rt],
                 ["yTs%d" % slot])

        B.pipeline([tile(n, s_) for n, s_ in enumerate(slots)], NBUF)
        B.end()

    def stage_merge1(self, do_ctx):
        B, d, l1 = self, self.d, do_ctx
        B.begin()
        wg = [B.sb("wg%d" % i, [128, KC, 3, 128], BF16) for i in range(2)]
        wb = [B.sb("wbr%d" % i, [128, 4, 3, 128], BF16) for i in range(2)]
        sg = [B.sb("sg%d" % i, [128, 512]) for i in range(3)]
        t1 = B.sb("t1", [128, 512])
        t2 = B.sb("t2", [128, 512])
        groups = []
        if l1:
            groups.append((0, 2))
        for g in range(4):
            groups.append((NCT + 4 * g, 4))
        yTs = [self.yTa, self.yTg, self.yTs]
        win = d["w_in"].rearrange("(k p) c -> p k c", p=128)
        for j in range(KC):
            g_ = wg[j % 2]
            b_ = wb[j % 2]
            gr = "wg%d" % (j % 2)
            br = "wbr%d" % (j % 2)
            for b3, c0 in enumerate((MA, MG, MS)):
                B.dma(g_[:, :, b3, :], win[:, :, c0 + j * 128:c0 + (j + 1) * 128], [], [gr + "_%d" % b3], eng="pool")
                B.dma(b_[:, :, b3, :], d["w_br"][b3].rearrange("(k p) c -> p k c", p=128)[:, :, j * 128:(j + 1) * 128], [], [br + "_%d" % b3], eng="pool")
            for (slot0, nsl) in groups:
                ntok = nsl * 128
                tok0 = slot0 * 128
                hres = ["hT%d" % s for s in range(slot0, slot0 + nsl)]
                pgs = []
                for b3 in range(3):
                    ps, rp = B.bank()
                    for k in range(KC):
                        B.mm(ps[:, 0:ntok], g_[:, k, b3, :], self.hT[:, k, tok0:tok0 + ntok], k == 0, k == KC - 1, [gr + "_%d" % b3], [rp])
                    B.act(sg[b3][:, 0:ntok], ps[:, 0:ntok], AF.Sigmoid, [rp], ["sg%d" % b3])
                pbs = []
                for b3 in range(3):
                    ps, rp = B.bank()
                    for k in range(4):
                        B.mm(ps[:, 0:ntok], b_[:, k, b3, :], yTs[b3][:, k, tok0:tok0 + ntok], k == 0, k == 3, [br + "_%d" % b3], [rp])
                    pbs.append((ps, rp))
                B.tt(t1[:, 0:ntok], pbs[0][0][:, 0:ntok], sg[0][:, 0:ntok], ALU.mult, [pbs[0][1], "sg0"], ["t1"])
                B.tt(t2[:, 0:ntok], pbs[1][0][:, 0:ntok], sg[1][:, 0:ntok], ALU.mult, [pbs[1][1], "sg1"], ["t2"])
                B.tt(t1[:, 0:ntok], t1[:, 0:ntok], t2[:, 0:ntok], ALU.add, ["t1", "t2"], ["t1"], eng="pool")
                B.tt(t2[:, 0:ntok], pbs[2][0][:, 0:ntok], sg[2][:, 0:ntok], ALU.mult, [pbs[2][1], "sg2", "t2"], ["t2"])
                B.tt(self.mT[:, j, tok0:tok0 + ntok], t1[:, 0:ntok], t2[:, 0:ntok], ALU.add, ["t1", "t2"], ["mT"], eng="pool")
        B.end()

    def stage_merge2(self, c, do_ctx):
        B, d, l1 = self, self.d, do_ctx
        dc = self.dcom
        B.begin()
        wo = B.sb("wo", [128, KC, D], BF16)
        B.dma(wo[:], d["w_out"].rearrange("(k p) c -> p k c", p=128), [], ["wo"], eng="pool")
        pg = B.sb("pg", [128, D])
        pb = B.sb("pb", [128, D])
        B.dma(pg[:], d["post_g"].broadcast_to([128, D]), [], ["pg"])
        B.dma(pb[:], d["post_b"].broadcast_to([128, D]), [], ["pb"])
        NBUF = 4
        sets = [dict(i=i, x=B.sb("mxt", [128, D]), r=B.sb("rr", [128, D]), st6=B.sb("mst6", [128, 2, 6]), mv=B.sb("mmv", [128, 4]))
                for i in range(NBUF)]
        slots = list(range(NS)) if l1 else list(range(NCT, NS))

        def tile(n, slot):
            bs = sets[n % NBUF]
            R = lambda nm: "%s%d" % (nm, bs["i"])
            x, r, st6, mv = bs["x"], bs["r"], bs["st6"], bs["mv"]
            which = 1 if slot < NCT else 0
            B.dma(x[:], self.src_rows(slot), [], [R("mxt")])
            for half in range(2):
                ps, rp = B.bank()
                for k in range(KC):
                    B.mm(ps[:, :], self.mT[:, k, slot * 128:(slot + 1) * 128], wo[:, k, half * 512:(half + 1) * 512], k == 0, k == KC - 1,
                         ["wo"], [rp])
                hs = slice(half * 512, (half + 1) * 512)
                hres = R("rr") + "h%d" % half
                B.tt(r[:, hs], ps[:, :], self.gateb[:, which, hs], ALU.mult, [rp, "gateb"], [hres])
                B.stt(r[:, hs], x[:, hs], ALPHA, r[:, hs], ALU.mult, ALU.add, [R("mxt"), hres], [hres])
                B.S.add("dve", lambda e, r=r, hs=hs, half=half: e.bn_stats(out=st6[:, half, :], in_=r[:, hs]),
                        reads=[hres], writes=[R("mst6_%d" % half)])
            yield
            B.S.add("dve", lambda e: e.bn_aggr(out=mv[:, 0:2], in_=st6[:].rearrange("p a b -> p (a b)")),
                    reads=[R("mst6_0"), R("mst6_1")], writes=[R("mmv")])
            B.ts(mv[:, 2:3], mv[:, 1:2], EPS, None, ALU.add, None, [R("mmv")], [R("mmv2")])
            B.act(mv[:, 3:4], mv[:, 2:3], AF.Ln, [R("mmv2")], [R("mmv3")])
            B.act(mv[:, 2:3], mv[:, 3:4], AF.Exp, [R("mmv3"), R("mmv2")], [R("mmv4")], scale=-0.5)
            yield
            rres = R("rr")
            B.ts(r[:], r[:], mv[:, 0:1], mv[:, 2:3], ALU.subtract, ALU.mult, [rres + "h0", rres + "h1", R("mmv"), R("mmv4")], [rres])
            B.tt(r[:], r[:], pg[:], ALU.mult, [rres, "pg"], [rres], eng="pool")
            B.tt(r[:], r[:], pb[:], ALU.add, [rres, "pb"], [rres], eng="pool")
            t = slot - NCT
            if slot < NCT:
                dst = dc["ctx1s"][slot * 128:(slot + 1) * 128, :]
            elif self.L == 0:
                dst = dc["x1s"][c, t * 128:(t + 1) * 128, :]
            else:
                dst = dc["x_out"][t * 128:(t + 1) * 128, :]
            B.dma(dst, r[:], [rres], [rres + "h0", rres + "h1"], eng="pool")

        B.pipeline([tile(n, s_) for n, s_ in enumerate(slots)], NBUF)
        B.end()


_NC_CACHE = {}


def _get_nc():
    if "f" not in _NC_CACHE:
        _NC_CACHE["f"] = Builder().build()
    return _NC_CACHE["f"]


def _rope_tables(tok0):
    t = np.arange(tok0, tok0 + NT * 128)
    row = (t // 64).astype(np.float32)
    col = (t % 64).astype(np.float32)
    freqs = (np.float32(10000.0) ** (-np.arange(0, 32, 2, dtype=np.float32) / np.float32(32))).astype(np.float32)
    ar = (row[:, None] * freqs[None, :]).astype(np.float32)
    ac = (col[:, None] * freqs[None, :]).astype(np.float32)
    cr, sr, cc, sc = np.cos(ar), np.sin(ar), np.cos(ac), np.sin(ac)
    cosF = np.concatenate([cr, cr, cc, cc], axis=1).astype(np.float32)
    sinF = np.concatenate([-sr, sr, -sc, sc], axis=1).astype(np.float32)

    def lay(a):
        return np.ascontiguousarray(a.reshape(NT, 128, 64).transpose(1, 0, 2))
    return lay(cosF), lay(sinF)


def _consts():
    c = np.zeros((128, 3, 128), np.float32)
    c[:, 0, :] = np.eye(128, dtype=np.float32)
    c[:, 1, :] = np.triu(np.ones((128, 128), np.float32))
    c[:, 2, :] = np.tril(np.ones((128, 128), np.float32))
    return c


_HPERM = [0, 4, 1, 5, 2, 6, 3, 7]


def _layout_w_in(w):
    o = np.zeros((D, WEND), np.float32)
    src = 0

    def take(n):
        nonlocal src
        s = w[:, src:src + n]
        src += n
        return s
    aq = take(512).reshape(D, 8, 64)[:, _HPERM, :].reshape(D, 512)
    ak = take(128)
    av = take(128)
    ag = take(512).reshape(D, 8, 64)[:, _HPERM, :].reshape(D, 512)
    o[:, AQ:AQ + 512] = aq
    o[:, AK:AK + 128] = ak
    o[:, AV:AV + 128] = av
    o[:, AG:AG + 512] = ag
    o[:, GQ:GQ + 256] = take(256)
    o[:, GK:GK + 256] = take(256)
    o[:, GV:GV + 512] = take(512)
    o[:, GA:GA + 16] = take(16)
    o[:, GA + 32:GA + 48] = take(16)
    o[:, GG:GG + 512] = take(512)
    o[:, SU:SU + 512] = take(512)
    o[:, SV:SV + 512] = take(512)
    o[:, SG:SG + 512] = take(512)
    o[:, MA:MA + 1024] = take(1024)
    o[:, MG:MG + 1024] = take(1024)
    o[:, MS:MS + 1024] = take(1024)
    assert src == 7456
    return o


def kernel(x, c, ctx, c_ctx, w_ada, b_ada, w_in, attn_q_norm, attn_k_norm,
           gla_a2_f, gla_ab_f, gla_a2_b, gla_ab_b, gla_o_norm,
           sgu_ln_g, sgu_ln_b, sgu_w, sgu_b,
           w_br_attn, w_br_gla, w_br_sgu, w_out, post_ln_g, post_ln_b):
    f = lambda a: np.ascontiguousarray(np.asarray(a, dtype=np.float32))
    x = f(x)
    ctx = f(ctx)
    c = f(c)
    c_ctx = f(c_ctx)
    ncores = 8
    consts = _consts()
    ropes = [_rope_tables(j * NT * 128) for j in range(4)]
    ropec_all = np.ascontiguousarray(np.concatenate([r[0] for r in ropes], axis=1))
    ropes_all = np.ascontiguousarray(np.concatenate([r[1] for r in ropes], axis=1))
    shared = dict(consts=consts, ropec_all=ropec_all, ropes_all=ropes_all)
    for l in range(2):
        sfx = "_%d" % l
        a2b = np.zeros((64, 256), np.float32)
        a2b[0:16] = f(gla_a2_f[l])
        a2b[16] = f(gla_ab_f[l])
        a2b[32:48] = f(gla_a2_b[l])
        a2b[48] = f(gla_ab_b[l])
        bl = f(b_ada[l])
        lay = dict(
            w_ada=f(w_ada[l]), b_ada_l=np.ascontiguousarray(bl[0:2048].reshape(16, 128).T),
            b_gate=np.ascontiguousarray(bl[2048:3072].reshape(1, D)), w_in=_layout_w_in(f(w_in[l])),
            k_norm=f(attn_k_norm[l]).reshape(1, 64), q_norm=f(attn_q_norm[l]).reshape(1, 64), a2b=a2b,
            o_norm=f(gla_o_norm[l]).reshape(1, 128), sgu_g=f(sgu_ln_g[l]).reshape(1, 512), sgu_b=f(sgu_ln_b[l]).reshape(1, 512),
            sgu_wT=np.ascontiguousarray(f(sgu_w[l]).transpose(0, 2, 1)), sgu_bT=np.ascontiguousarray(f(sgu_b[l]).T),
            w_br=np.stack([f(w_br_attn[l]).reshape(8, 64, D)[_HPERM].reshape(512, D), f(w_br_gla[l]), f(w_br_sgu[l])], axis=0),
            w_out=f(w_out[l]), post_g=f(post_ln_g[l]).reshape(1, D), post_b=f(post_ln_b[l]).reshape(1, D))
        for k, v in lay.items():
            shared[k + sfx] = v
    maps = []
    for cid in range(ncores):
        b, j = cid // 4, cid % 4
        cv = np.stack([c[b], c_ctx], axis=1)
        m = dict(shared)
        m.update(x=np.ascontiguousarray(x[b].reshape(4, NT * 128, D)), ctx=np.ascontiguousarray(ctx[b]),
                 cvec=np.ascontiguousarray(cv.reshape(KC, 128, 2).transpose(1, 0, 2)),
                 ropec_own=ropes[j][0], ropes_own=ropes[j][1])
        maps.append(m)
    res = run_bass_kernel_spmd(_get_nc(), maps, core_ids=list(range(ncores))).results
    out = np.zeros((2, 8192, D), np.float32)
    for cid in range(ncores):
        out[cid // 4, (cid % 4) * 2048:(cid % 4 + 1) * 2048, :] = np.asarray(res[cid]["x_out"], np.float32)
    return out
```
